# Optimizing a Trainium2 kernel written in Bass

```python
import math
import jax, jax.numpy as jnp
from jax import lax
import numpy as np

D_MODEL = 1024
BATCH = 8
SEQ = 4096
DEPTH = 1

CHUNK = 64
D_MIX = D_MODEL
D_ATT = D_MIX // 2
N_ATT_HEADS = 8
ATT_HEAD_DIM = D_ATT // N_ATT_HEADS
LEFT_CHUNKS = 8
BAND = (LEFT_CHUNKS + 1) * CHUNK
REL_CLIP = 128
D_GLA_V = D_MIX - D_ATT
N_GLA_HEADS = 4
D_GLA_K = D_GLA_V // 2
GLA_HEAD_K = D_GLA_K // N_GLA_HEADS
GLA_HEAD_V = D_GLA_V // N_GLA_HEADS
GLA_LOW_RANK = 16
GLA_TAU = 16.0
D_PLE = 256
LN_EPS = 1e-5
RMS_EPS = 1e-6
DEEPNORM_ALPHA = (2.0 * DEPTH) ** 0.25
DEEPNORM_BETA = (8.0 * DEPTH) ** -0.25
D_IN_PROJ = 4 * D_ATT + 2 * D_GLA_K + 2 * D_GLA_V + GLA_LOW_RANK

kernel_name = "hymba_chunked_attn_gla_deepnorm"


def _split_sizes():
    return (D_ATT, D_ATT, D_ATT, D_ATT, D_GLA_K, D_GLA_K, D_GLA_V, D_GLA_V, GLA_LOW_RANK)


def layer_norm(x, g, b):
    xf = x.astype(jnp.float32)
    mu = jnp.mean(xf, axis=-1, keepdims=True)
    xc = xf - mu
    var = jnp.mean(xc * xc, axis=-1, keepdims=True)
    y = xc * lax.rsqrt(var + LN_EPS) * g.astype(jnp.float32) + b.astype(jnp.float32)
    return y.astype(x.dtype)


def chunk_band_attention(q, k, v, rel_table):
    b, s, h, dh = q.shape
    nc = s // CHUNK
    q = jnp.transpose(q, (0, 2, 1, 3))
    k = jnp.transpose(k, (0, 2, 1, 3))
    v = jnp.transpose(v, (0, 2, 1, 3))
    left = LEFT_CHUNKS * CHUNK
    k_pad = jnp.pad(k, ((0, 0), (0, 0), (left, 0), (0, 0)))
    v_pad = jnp.pad(v, ((0, 0), (0, 0), (left, 0), (0, 0)))
    q_chunks = jnp.moveaxis(q.reshape(b, h, nc, CHUNK, dh), 2, 0)
    qi = jnp.arange(CHUNK)[:, None]
    kj = jnp.arange(BAND)[None, :]
    rel = qi + left - kj
    rel_idx = jnp.clip(rel, -REL_CLIP, REL_CLIP) + REL_CLIP
    bias = rel_table.astype(jnp.float32)[:, rel_idx]
    scale = ATT_HEAD_DIM ** -0.5

    def one_chunk(args):
        c, q_blk = args
        start = c * CHUNK
        k_blk = lax.dynamic_slice_in_dim(k_pad, start, BAND, axis=2)
        v_blk = lax.dynamic_slice_in_dim(v_pad, start, BAND, axis=2)
        sc = jnp.einsum('bhqd,bhkd->bhqk', q_blk, k_blk).astype(jnp.float32) * scale + bias
        valid = (start - left + kj) >= 0
        sc = jnp.where(valid, sc, -1e30)
        pr = jax.nn.softmax(sc, axis=-1)
        return jnp.einsum('bhqk,bhkd->bhqd', pr.astype(v_blk.dtype), v_blk)

    out = lax.map(one_chunk, (jnp.arange(nc), q_chunks))
    out = jnp.transpose(out, (1, 0, 3, 2, 4))
    return out.reshape(b, s, h * dh)


def gla_chunked(q, k, v, log_a):
    b, s, h, dk = q.shape
    dv = v.shape[-1]
    nc = s // CHUNK
    f32 = jnp.float32
    qf = q.astype(f32).reshape(b, nc, CHUNK, h, dk) * (dk ** -0.5)
    kf = k.astype(f32).reshape(b, nc, CHUNK, h, dk)
    vf = v.astype(f32).reshape(b, nc, CHUNK, h, dv)
    L = jnp.cumsum(log_a.astype(f32).reshape(b, nc, CHUNK, h, dk), axis=2)
    L_end = L[:, :, -1:]
    eL = jnp.exp(L)
    e_negL = jnp.exp(-L)
    q_fwd = qf * eL
    a_causal = jnp.einsum('bnthd,bnshd->bnhts', q_fwd, kf * e_negL)
    a_anti = jnp.einsum('bnthd,bnshd->bnhts', qf * e_negL, kf * eL)
    tril = jnp.tril(jnp.ones((CHUNK, CHUNK), dtype=bool))
    att = jnp.where(tril, a_causal, a_anti)
    o_intra = jnp.einsum('bnhts,bnshv->bnthv', att, vf)
    kv = jnp.einsum('bnshd,bnshv->bnhdv', kf * jnp.exp(L_end - L), vf)
    chunk_decay = jnp.exp(L_end[:, :, 0])

    def step(state, inp):
        dec, kv_c = inp
        return dec[..., None] * state + kv_c, state

    init = jnp.zeros((b, h, dk, dv), f32)
    _, s_prev = lax.scan(step, init, (jnp.moveaxis(chunk_decay, 1, 0), jnp.moveaxis(kv, 1, 0)))
    s_prev = jnp.moveaxis(s_prev, 0, 1)
    o_inter = jnp.einsum('bnthd,bnhdv->bnthv', q_fwd, s_prev)
    return (o_intra + o_inter).reshape(b, s, h, dv)


def setup_inputs(seed: int = 0) -> dict:
    key = jax.random.key(seed)
    ks = jax.random.split(key, 20)
    f32 = jnp.float32
    n = jax.random.normal
    x = n(ks[0], (BATCH, SEQ, D_MODEL), f32)
    p = n(ks[1], (DEPTH, BATCH, SEQ, D_PLE), f32)
    ln_in_g = 1.0 + 0.02 * n(ks[2], (D_MODEL,), f32)
    ln_in_b = 0.02 * n(ks[3], (D_MODEL,), f32)
    w_in = n(ks[4], (DEPTH, D_MODEL, D_IN_PROJ), f32) * D_MODEL ** -0.5
    w_gla_gate = n(ks[5], (DEPTH, GLA_LOW_RANK, D_GLA_K), f32) * GLA_LOW_RANK ** -0.5
    b_gla_gate = 0.1 * n(ks[6], (DEPTH, D_GLA_K), f32) + 1.0
    rel_bias = 0.1 * n(ks[7], (DEPTH, N_ATT_HEADS, 2 * REL_CLIP + 1), f32)
    gla_norm_g = 1.0 + 0.02 * n(ks[8], (DEPTH, N_GLA_HEADS, GLA_HEAD_V), f32)
    w_out = n(ks[9], (DEPTH, D_MIX, D_MODEL), f32) * (D_MIX ** -0.5) * DEEPNORM_BETA
    w_ple = n(ks[10], (DEPTH, D_PLE, D_MODEL), f32) * D_PLE ** -0.5
    w_ple_gate = n(ks[11], (DEPTH, D_MODEL, D_MODEL), f32) * D_MODEL ** -0.5
    b_ple_gate = 0.02 * n(ks[12], (DEPTH, D_MODEL), f32)
    ln_g = 1.0 + 0.02 * n(ks[13], (DEPTH, D_MODEL), f32)
    ln_b = 0.02 * n(ks[14], (DEPTH, D_MODEL), f32)
    return {"x": x, "p": p, "ln_in_g": ln_in_g, "ln_in_b": ln_in_b, "w_in": w_in,
            "w_gla_gate": w_gla_gate, "b_gla_gate": b_gla_gate, "rel_bias": rel_bias,
            "gla_norm_g": gla_norm_g, "w_out": w_out, "w_ple": w_ple,
            "w_ple_gate": w_ple_gate, "b_ple_gate": b_ple_gate, "ln_g": ln_g, "ln_b": ln_b}


def reference(x, p, ln_in_g, ln_in_b, w_in, w_gla_gate, b_gla_gate, rel_bias,
              gla_norm_g, w_out, w_ple, w_ple_gate, b_ple_gate, ln_g, ln_b):
    b, s, _ = x.shape
    split_points = np.cumsum(np.array(_split_sizes()))[:-1].tolist()
    h = layer_norm(x, ln_in_g, ln_in_b)
    for i in range(DEPTH):
        proj = h @ w_in[i]
        (aq, ak, av, ag, gq, gk, gv, gg, glr) = jnp.split(proj, split_points, axis=-1)
        att = chunk_band_attention(
            aq.reshape(b, s, N_ATT_HEADS, ATT_HEAD_DIM),
            ak.reshape(b, s, N_ATT_HEADS, ATT_HEAD_DIM),
            av.reshape(b, s, N_ATT_HEADS, ATT_HEAD_DIM),
            rel_bias[i])
        att = att * jax.nn.silu(ag)
        gate_logit = (glr @ w_gla_gate[i] + b_gla_gate[i]).astype(jnp.float32)
        log_a = jax.nn.log_sigmoid(gate_logit) / GLA_TAU
        o = gla_chunked(
            gq.reshape(b, s, N_GLA_HEADS, GLA_HEAD_K),
            gk.reshape(b, s, N_GLA_HEADS, GLA_HEAD_K),
            gv.reshape(b, s, N_GLA_HEADS, GLA_HEAD_V),
            log_a.reshape(b, s, N_GLA_HEADS, GLA_HEAD_K))
        o = o * lax.rsqrt(jnp.mean(o * o, axis=-1, keepdims=True) + RMS_EPS) \
            * gla_norm_g[i].astype(jnp.float32)
        gla = o.reshape(b, s, D_GLA_V).astype(h.dtype) * jax.nn.silu(gg)
        mix = jnp.concatenate([att.astype(h.dtype), gla], axis=-1) @ w_out[i]
        r = DEEPNORM_ALPHA * h + mix
        ple_gate = jax.nn.sigmoid(r @ w_ple_gate[i] + b_ple_gate[i])
        r = r + ple_gate * (p[i] @ w_ple[i])
        h = layer_norm(r, ln_g[i], ln_b[i])
    return h
```

```python
import contextlib
import numpy as np
import concourse.bass as bass
import concourse.mybir as mybir
from concourse.bass_utils import run_bass_kernel_spmd

F32 = mybir.dt.float32
BF16 = mybir.dt.bfloat16
AF = mybir.ActivationFunctionType
ALU = mybir.AluOpType

D = 1024
DP = 3600
NCORES = 8
ALPHA = 2.0 ** 0.25
LN_EPS = 1e-5
RMS_EPS = 1e-6

SCHED = {"D": (0.0, 0.67), "B": (0.0, 0.8), "C": (0.04, 0.7), "A": (0.1, 0.79), "L": (0.16, 0.45)}


class _Op:
    __slots__ = ("eng", "fn", "deps", "needed", "count", "dsem", "dcount", "is_dma")

    def __init__(self, eng, fn):
        self.eng = eng
        self.fn = fn
        self.deps = []
        self.needed = False
        self.count = None
        self.dsem = None
        self.dcount = None
        self.is_dma = False


class Prog:
    ENGS = ("pe", "act", "dve", "pool", "sp")

    def __init__(self):
        self.ops = {e: [] for e in self.ENGS}
        self.lastw = {}
        self.readers = {}
        self.dma_keys = {}

    def op(self, eng, fn, r=(), w=(), dma=None):
        o = _Op(eng, fn)
        deps = []
        seen = set()

        def add(d):
            if d is o or id(d) in seen:
                return
            if d.eng == "pe" and eng == "pe" and not d.is_dma and dma is None:
                return
            seen.add(id(d))
            deps.append(d)

        for k in r:
            d = self.lastw.get(k)
            if d is not None:
                add(d)
        for k in w:
            d = self.lastw.get(k)
            if d is not None:
                add(d)
            for rd in self.readers.get(k, ()):
                add(rd)
        o.deps = deps
        for d in deps:
            d.needed = True
        for k in r:
            self.readers.setdefault(k, []).append(o)
        for k in w:
            self.lastw[k] = o
            self.readers[k] = []
        if dma is not None:
            o.is_dma = True
            o.dsem = dma
            c = self.dma_keys.get(dma, 0) + 16
            self.dma_keys[dma] = c
            o.dcount = c
        self.ops[eng].append(o)
        return o

    def emit(self, nc, final_waits=()):
        with contextlib.ExitStack() as st:
            esem = {e: st.enter_context(nc.semaphore("s_" + e)) for e in self.ENGS}
            dsem = {k: st.enter_context(nc.semaphore("d_" + str(k))) for k in self.dma_keys}
            for e in self.ENGS:
                c = 0
                for o in self.ops[e]:
                    if not o.is_dma and o.needed:
                        c += 1
                        o.count = c
            block = st.enter_context(nc.Block())

            def signal(d):
                if d.is_dma:
                    return dsem[d.dsem], d.dcount
                return esem[d.eng], d.count

            def run(ename, engine):
                waited = {}
                for o in self.ops[ename]:
                    for d in o.deps:
                        s, v = signal(d)
                        if waited.get(id(s), 0) >= v:
                            continue
                        waited[id(s)] = v
                        engine.wait_ge(s, v)
                    ins = o.fn(engine)
                    if o.is_dma:
                        ins.then_inc(dsem[o.dsem], 16)
                    elif o.needed:
                        ins.then_inc(esem[ename], 1)
                if ename == "sp":
                    for d in final_waits:
                        s, v = signal(d)
                        if waited.get(id(s), 0) >= v:
                            continue
                        waited[id(s)] = v
                        engine.wait_ge(s, v)

            @block.tensor
            def _(e):
                run("pe", e)

            @block.scalar
            def _(e):
                run("act", e)

            @block.vector
            def _(e):
                run("dve", e)

            @block.gpsimd
            def _(e):
                run("pool", e)

            @block.sync
            def _(e):
                run("sp", e)


def build_nc(NT, dbg=None):
    nc = bass.Bass("TRN2", target_bir_lowering=False)
    T = NT * 128

    def din(name, shape):
        return nc.dram_tensor(name, list(shape), F32, kind="ExternalInput").ap()

    x_d = din("x", [T, D])
    p_d = din("p", [T, 256])
    win_d = din("w_in", [D, DP])
    wout_d = din("w_out", [D, D])
    wgate_d = din("w_gate", [D, D])
    wple_d = din("w_ple", [256, D])
    vec_d = din("vecs", [4, 128, D])
    gnorm_d = din("gnorm", [128, 4])
    bgate_d = din("bgate", [1, D])
    wgg_d = din("wgg", [17, 256])
    bias_d = din("biasT", [128, 2, 8, 128])
    cfar_d = din("cfar", [128, 8])
    cst_d = din("consts", [3, 128, 128])
    cst2_d = din("consts2", [2, 128, 128])
    out_d = nc.dram_tensor("out", [T, D], F32, kind="ExternalOutput").ap()

    st = contextlib.ExitStack()

    def sb(name, shape, dt=F32):
        return st.enter_context(nc.sbuf_tensor(name, list(shape), dt))

    with st:
        win = sb("win", [128, 8, DP], BF16)
        wout = sb("wout", [128, 8, D], BF16)
        wgate = sb("wgate", [128, 8, D], BF16)
        wple = sb("wple", [128, 2, D], BF16)
        vecs = sb("vecs_sb", [128, 2, D])
        gnorm = sb("gnorm_sb", [128, 4])
        bg32 = sb("bg32", [1, D])
        bghi = sb("bghi", [1, D], BF16)
        bglo = sb("bglo", [1, D], BF16)
        bgt = sb("bgt", [1, D])
        ones_bf = sb("ones_bf", [1, 128], BF16)
        wgg = sb("wgg_sb", [17, 256])
        biasT = sb("biasT_sb", [128, 2, 8, 128])
        cfar = sb("cfar_sb", [128, 8])
        cst = sb("cst_sb", [128, 3, 128])
        ident = sb("ident", [128, 128], BF16)

        NX = 4
        xt = [sb("xt%d" % i, [128, D]) for i in range(NX)]
        pt = [sb("pt%d" % i, [128, 256]) for i in range(1)]
        tokbf = sb("tokbf", [128, D], BF16)
        featT = sb("featT", [128, 8, 128], BF16)
        hbf = sb("hbf", [128, D], BF16)
        rbf = sb("rbf", [128, D], BF16)
        hT = sb("hT", [128, 8, 128], BF16)
        stats = sb("stats", [128, 2, 6])
        mv = sb("mv", [128, 2])
        rstd = sb("rstd", [128, 1])
        nmr = sb("nmr", [128, 1])
        nmr2 = sb("nmr2", [128, 1])
        qtok = sb("qtok", [128, 512], BF16)
        ktok = sb("ktok", [128, 512], BF16)
        qT = sb("qT", [128, 4, 128], BF16)
        NR = 5
        kT = [sb("kT%d" % i, [128, 4, 128], BF16) for i in range(NR)]
        NRV = 6
        vext = [sb("vext%d" % i, [128, 8, 65], BF16) for i in range(NRV)]
        sila2 = [sb("sila%d" % i, [128, 512]) for i in range(2)]
        silg2 = [sb("silg%d" % i, [128, 512]) for i in range(2)]
        gqk2 = [sb("gqk%d" % i, [128, 512]) for i in range(2)]
        vtok2 = [sb("vtok%d" % i, [128, 512], BF16) for i in range(2)]
        glrT2 = [sb("glrT%d" % i, [17, 128]) for i in range(2)]
        nla = sb("nla", [128, 256])
        E1 = sb("E1", [128, 512])
        eEnd = sb("eEnd", [128, 256])
        dec = sb("dec", [64, 8])
        qfkn = sb("qfkn", [128, 512], BF16)
        qakp = sb("qakp", [128, 512], BF16)
        ke = sb("ke", [128, 256], BF16)
        GT1 = sb("GT1", [64, 8, 128], BF16)
        GT2 = sb("GT2", [64, 8, 128], BF16)
        QA = sb("QA", [128, 4, 128], BF16)
        QB = sb("QB", [128, 4, 128], BF16)
        PT = sb("PT", [128, 5, 4, 128], BF16)
        rcp = sb("rcp", [128, 4])
        attn_t = sb("attn_t", [128, 256])
        attT = sb("attT", [128, 4, 128], BF16)
        Sst = sb("Sst", [64, 4, 128])
        Sbf = [sb("Sbf%d" % i, [128, 4, 128], BF16) for i in range(3)]
        osb = E1
        sqj = qakp
        att2 = qfkn[:].rearrange("p (h t) -> p h t", h=4)
        ssq = sb("ssq", [128, 4])
        rms = sb("rms", [128, 4])
        cat = tokbf
        catT = featT
        eg = sb("eg", [128, 512])
        pbf = sb("pbf", [128, 256], BF16)
        pT = sb("pT", [128, 2, 128], BF16)
        stats2 = sb("stats2", [128, 2, 6])
        mv2 = sb("mv2", [128, 2])
        rstd2 = sb("rstd2", [128, 1])

        NFB = 6
        fb = [st.enter_context(nc.psum_tensor("fb%d" % i, [128, 512], F32)) for i in range(NFB)]
        tb = [st.enter_context(nc.psum_tensor("tb%d" % i, [128, 1024], BF16)) for i in range(2)]
        fctr = [0]
        tctr = [0]

        def fbank():
            i = fctr[0] % NFB
            fctr[0] += 1
            return fb[i], "fb%d" % i

        def tbank():
            i = tctr[0] % 2
            tctr[0] += 1
            return tb[i], "tb%d" % i

        P = Prog()
        op = P.op

        op("sp", lambda e: e.dma_start(out=cst[:], in_=cst_d.rearrange("c p f -> p c f")), w=["cst"], dma="cst")
        op("sp", lambda e: e.dma_start(out=vecs[:, 0, :], in_=vec_d[0]), w=["vecs"], dma="vecs")
        op("sp", lambda e: e.dma_start(out=vecs[:, 1, :], in_=vec_d[2]), w=["vecs"], dma="vecs")
        op("sp", lambda e: e.dma_start(out=gnorm[:], in_=gnorm_d), w=["gnorm"], dma="gnorm")
        op("sp", lambda e: e.dma_start(out=bg32[:], in_=bgate_d), w=["bg32"], dma="bg32")
        op("sp", lambda e: e.dma_start(out=wgg[:], in_=wgg_d), w=["wgg"], dma="wgg")
        op("sp", lambda e: e.dma_start(out=biasT[:], in_=bias_d), w=["biasT"], dma="biasT")
        op("sp", lambda e: e.dma_start(out=cfar[:], in_=cfar_d), w=["cfar"], dma="cfar")
        op("sp", lambda e: e.dma_start(out=xt[2][:, 0:256].rearrange("p (c f) -> p c f", c=2), in_=cst2_d.rearrange("c p f -> p c f")), w=["xt2"], dma="xt2")
        op("dve", lambda e: e.tensor_copy(out=ident[:], in_=xt[2][:, 0:128]), r=["xt2"], w=["ident"])
        Mle = cst[:, 0, :]
        Mgt = cst[:, 1, :]
        maskneg = xt[2][:, 128:256]
        chunk_ind = cst[:, 2, 0:2]
        for b_ in range(2):
            op("dve", (lambda b_: (lambda e: e.tensor_tensor(
                out=biasT[:, b_, :, :], in0=biasT[:, b_, :, :],
                in1=cfar[:, :].unsqueeze(2).to_broadcast([128, 8, 128]), op=ALU.subtract)))(b_), r=["biasT", "cfar"], w=["biasT"])
        op("dve", lambda e: e.tensor_tensor(
            out=biasT[:, 1, :, :], in0=biasT[:, 1, :, :],
            in1=maskneg.unsqueeze(1).to_broadcast([128, 8, 128]), op=ALU.add), r=["biasT", "xt2"], w=["biasT"])
        op("dve", lambda e: e.tensor_copy(out=bghi[:], in_=bg32[:]), r=["bg32"], w=["bghi"])
        op("dve", lambda e: e.tensor_copy(out=bgt[:], in_=bghi[:]), r=["bghi"], w=["bgt"])
        op("dve", lambda e: e.tensor_tensor(out=bgt[:], in0=bg32[:], in1=bgt[:], op=ALU.subtract), r=["bg32", "bgt"], w=["bgt"])
        op("dve", lambda e: e.tensor_copy(out=bglo[:], in_=bgt[:]), r=["bgt"], w=["bglo"])
        op("dve", lambda e: e.memset(ones_bf[:], 1.0), w=["ones_bf"])
        for i in range(2):
            op("pool", (lambda t: (lambda e: e.memset(t[:], 1.0)))(glrT2[i]), w=["glrT%d" % i])
        op("pool", lambda e: e.memset(Sst[:], 0.0), w=["Sst"])
        for i in range(3):
            op("pool", (lambda t: (lambda e: e.memset(t[:], 0.0)))(Sbf[i]), w=["Sbf%d" % i])
        op("pool", lambda e: e.memset(QA[:], 0.0), w=["QA"])
        op("pool", lambda e: e.memset(QB[:], 0.0), w=["QB"])
        for i in range(NRV):
            op("pool", (lambda t: (lambda e: e.memset(t[:], 1.0)))(vext[i]), w=["vext%d" % i])

        cast_rr = [0]

        def load_w(dst, src_d, nchunk, ncols, wk, rowscale=None):
            for c in range(nchunk):
                c0 = 0
                while c0 < ncols:
                    cw = min(1024, ncols - c0)
                    s = cast_rr[0] % NX
                    eng = ("dve", "act")[cast_rr[0] % 2]
                    cast_rr[0] += 1
                    op("sp", (lambda s, c, c0, cw: (lambda e: e.dma_start(
                        out=xt[s][:, 0:cw], in_=src_d[c * 128:(c + 1) * 128, c0:c0 + cw])))(s, c, c0, cw),
                       w=["xt%d" % s], dma="xt%d" % s)
                    if rowscale is not None and c in rowscale:
                        eng = "dve"
                        f = (lambda s, c, c0, cw, sc: (lambda e: e.tensor_scalar_mul(out=dst[:, c, c0:c0 + cw], in0=xt[s][:, 0:cw], scalar1=sc)))(s, c, c0, cw, rowscale[c])
                    elif eng == "act":
                        f = (lambda s, c, c0, cw: (lambda e: e.activation(out=dst[:, c, c0:c0 + cw], in_=xt[s][:, 0:cw], func=AF.Copy)))(s, c, c0, cw)
                    else:
                        f = (lambda s, c, c0, cw: (lambda e: e.tensor_copy(out=dst[:, c, c0:c0 + cw], in_=xt[s][:, 0:cw])))(s, c, c0, cw)
                    op(eng, f, r=["xt%d" % s, "gnorm"], w=[wk])
                    c0 += cw

        KWIN, KWOUT, KWGATE, KWPLE = "win", "wout", "wgate", "wple"
        KWIN_KEYS = []
        for c in range(8):
            for c0 in (0, 1200, 2400):
                k_ = "win_%d_%d" % (c, c0)
                KWIN_KEYS.append(k_)
                op("pool", (lambda c, c0: (lambda e: e.dma_start(out=win[:, c, c0:c0 + 1200], in_=win_d[c * 128:(c + 1) * 128, c0:c0 + 1200])))(c, c0),
                   w=[k_], dma="dwin")

        KLATE = {"wout": "wout_3", "wgate": "wgate_7", "wple": "wple_1"}

        def emit_late():
            for dst, src_d, chunks, nm_ in ((wout, wout_d, range(4), "wout"), (wgate, wgate_d, range(8), "wgate"), (wple, wple_d, range(2), "wple")):
                for c in chunks:
                    k_ = "%s_%d" % (nm_, c)
                    op("pool", (lambda dst, src_d, c: (lambda e: e.dma_start(out=dst[:, c, :], in_=src_d[c * 128:(c + 1) * 128, :])))(dst, src_d, c),
                       r=[KWIN_KEYS[-1]], w=[k_], dma="d" + nm_)

        def gen_W():
            stg = [(eg[:, :], "eg"), (rbf[:, :].bitcast(F32), "rbf"), (featT[:].rearrange("p c t -> p (c t)").bitcast(F32), "featT")]
            rowscale = {4 + h: gnorm[:, h:h + 1] for h in range(4)}
            n = 0
            for dst, src_d, nchunk, wk in ((wout, wout_d, 8, KWOUT),):
                for c in range(4, nchunk):
                    for c0 in (0, 512):
                        sap, sk = stg[n % 3]
                        eng = ("dve", "act")[n % 2]
                        n += 1
                        op("sp", (lambda sap, src_d, c, c0: (lambda e: e.dma_start(out=sap, in_=src_d[c * 128:(c + 1) * 128, c0:c0 + 512])))(sap, src_d, c, c0),
                           w=[sk], dma="w" + sk)
                        if wk == KWOUT and c in rowscale:
                            f = (lambda sap, dst, c, c0, sc: (lambda e: e.tensor_scalar_mul(out=dst[:, c, c0:c0 + 512], in0=sap, scalar1=sc)))(sap, dst, c, c0, rowscale[c])
                            eng = "dve"
                        elif eng == "act":
                            f = (lambda sap, dst, c, c0: (lambda e: e.activation(out=dst[:, c, c0:c0 + 512], in_=sap, func=AF.Copy)))(sap, dst, c, c0)
                        else:
                            f = (lambda sap, dst, c, c0: (lambda e: e.tensor_copy(out=dst[:, c, c0:c0 + 512], in_=sap)))(sap, dst, c, c0)
                        op(eng, f, r=[sk, "gnorm"], w=[wk])
                        yield 1.0

        out_dmas = []

        def make_tile(i):
            xs = i % NX
            kslot = i % NR
            XT, PTt = xt[xs], pt[0]
            kx, kp_ = "xt%d" % xs, "pt0"
            hb = i % 2
            sila, silg, gqk, vtok, glrT = sila2[hb], silg2[hb], gqk2[hb], vtok2[hb], glrT2[hb]
            ks_sila, ks_silg, ks_gqk, ks_vtok, ks_glrT = "sila%d" % hb, "silg%d" % hb, "gqk%d" % hb, "vtok%d" % hb, "glrT%d" % hb
            vslot = i % NRV
            RR = XT
            krr = kx
            hf = XT

            def f_tr8(e, tbk=None, src=hbf):
                for c in range(8):
                    ins = e.transpose(out=tbk[:, c * 128:(c + 1) * 128], in_=src[:, c * 128:(c + 1) * 128], identity=ident[:])
                return ins
            def inproj(n0, ncols):
                bk, kb = fbank()
                def f(e, bk=bk):
                    for c in range(8):
                        ins = e.matmul(bk[:, 0:ncols], lhsT=hT[:, c, :], rhs=win[:, c, n0:n0 + ncols], start=(c == 0), stop=(c == 7))
                    return ins
                op("pe", f, r=["hT", KWIN_KEYS[-1]], w=[kb])
                return bk, kb

            def loads():
                op("sp", lambda e: e.dma_start(out=XT[:], in_=x_d[i * 128:(i + 1) * 128, :]), w=[kx], dma=kx)

            def loads_p():
                op("sp", lambda e: e.dma_start(out=PTt[:], in_=p_d[i * 128:(i + 1) * 128, :]), w=[kp_], dma=kp_)

            def stage_L():
                def f_stats(e):
                    e.bn_stats(out=stats[:, 0, :], in_=XT[:, 0:512])
                    return e.bn_stats(out=stats[:, 1, :], in_=XT[:, 512:1024])
                op("dve", f_stats, r=[kx], w=["stats"])
                op("dve", lambda e: e.bn_aggr(out=mv[:], in_=stats[:]), r=["stats"], w=["mv"])
                op("dve", lambda e: e.tensor_scalar_mul(out=nmr[:], in0=mv[:, 0:1], scalar1=-1.0), r=["mv"], w=["nmr"])
                yield 1.0
                op("act", lambda e: e.activation(out=rstd[:], in_=mv[:, 1:2], func=AF.Ln, bias=LN_EPS, scale=1.0), r=["mv"], w=["rstd"])
                op("act", lambda e: e.activation(out=rstd[:], in_=rstd[:], func=AF.Exp, scale=-0.5), w=["rstd"])
                op("act", lambda e: e.activation(out=nmr[:], in_=nmr[:], func=AF.Identity, scale=rstd[:, 0:1]), r=["rstd"], w=["nmr"])
                op("act", lambda e: e.activation(out=XT[:], in_=XT[:], func=AF.Identity, scale=rstd[:, 0:1], bias=nmr[:, 0:1]), r=["rstd", "nmr"], w=[kx])
                yield 1.0
                op("pool", lambda e: e.tensor_tensor(out=XT[:], in0=XT[:], in1=vecs[:, 0, :], op=ALU.mult), r=["vecs"], w=[kx])
                yield 1.0
                op("pool", lambda e: e.dma_start(out=XT[:], in_=vec_d[1], accum_op=ALU.add), w=[kx], dma="acc" + kx)
                yield 1.5
                op("pool", lambda e: e.tensor_copy(out=hbf[:], in_=XT[:]), r=[kx], w=["hbf"])
                yield 0.5

            def gen_A1():
                tbk, ktb = tbank()
                op("pe", (lambda tbk: (lambda e: f_tr8(e, tbk=tbk, src=hbf)))(tbk), r=["hbf", "ident"], w=[ktb])
                op("act", (lambda tbk: (lambda e: e.activation(out=hT[:].rearrange("p c t -> p (c t)"), in_=tbk[:, :], func=AF.Copy)))(tbk), r=[], w=[ktb, "hT"])
                yield 0.6

                bk, kb = inproj(0, 512)
                op("act", (lambda bk: (lambda e: e.activation(out=qtok[:], in_=bk[:, :], func=AF.Identity, scale=0.125)))(bk), w=[kb, "qtok"])
                yield 1.7
                bk, kb = inproj(512, 512)
                op("dve", (lambda bk: (lambda e: e.tensor_copy(out=ktok[:], in_=bk[:, :])))(bk), w=[kb, "ktok"])
                yield 1.7

            def gen_A2():
                bk, kb = inproj(1024, 512)
                op("act", (lambda bk, V: (lambda e: e.activation(out=V[:, :, 0:64], in_=bk[:, :].rearrange("p (h d) -> p h d", h=8), func=AF.Copy)))(bk, vext[vslot]),
                   w=[kb, "vext%d" % vslot])
                yield 1.7
                bk, kb = inproj(1536, 512)
                op("act", (lambda bk: (lambda e: e.activation(out=sila[:], in_=bk[:, :], func=AF.Exp, scale=-1.0)))(bk), w=[kb, ks_sila])
                op("act", lambda e: e.activation(out=sila[:], in_=sila[:], func=AF.Ln, bias=1.0, scale=1.0), w=[ks_sila])
                op("act", lambda e: e.activation(out=sila[:], in_=sila[:], func=AF.Exp, scale=-1.0), w=[ks_sila])
                op("dve", (lambda bk: (lambda e: e.tensor_tensor(out=sila[:], in0=bk[:, :], in1=sila[:], op=ALU.mult)))(bk), w=[kb, ks_sila])
                yield 1.7
                bk, kb = inproj(2048, 512)
                op("act", (lambda bk: (lambda e: e.activation(out=gqk[:, 0:256], in_=bk[:, 0:256], func=AF.Identity, scale=0.125)))(bk), w=[kb, ks_gqk])
                op("act", (lambda bk: (lambda e: e.activation(out=gqk[:, 256:512], in_=bk[:, 256:512], func=AF.Copy)))(bk), w=[kb, ks_gqk])
                yield 1.7
                bk, kb = inproj(2560, 512)
                op("dve", (lambda bk: (lambda e: e.tensor_copy(out=vtok[:], in_=bk[:, :])))(bk), w=[kb, ks_vtok])
                yield 1.7
                bk, kb = inproj(3072, 512)
                op("act", (lambda bk: (lambda e: e.activation(out=silg[:], in_=bk[:, :], func=AF.Exp, scale=-1.0)))(bk), w=[kb, ks_silg])
                op("act", lambda e: e.activation(out=silg[:], in_=silg[:], func=AF.Ln, bias=1.0, scale=1.0), w=[ks_silg])
                op("act", lambda e: e.activation(out=silg[:], in_=silg[:], func=AF.Exp, scale=-1.0), w=[ks_silg])
                op("dve", (lambda bk: (lambda e: e.tensor_tensor(out=silg[:], in0=bk[:, :], in1=silg[:], op=ALU.mult)))(bk), w=[kb, ks_silg])
                yield 1.7
                bk, kb = fbank()
                def f_glr(e, bk=bk):
                    for c in range(8):
                        ins = e.matmul(bk[0:16, 0:128], lhsT=win[:, c, 3584:3600], rhs=hT[:, c, :], start=(c == 0), stop=(c == 7))
                    return ins
                op("pe", f_glr, r=["hT", KWIN_KEYS[-1]], w=[kb])
                op("act", (lambda bk: (lambda e: e.activation(out=glrT[0:16, :], in_=bk[0:16, 0:128], func=AF.Copy)))(bk), w=[kb, ks_glrT])
                yield 0.6

            def gen_B():
                tbk, ktb = tbank()
                def f_trq(e, tbk=tbk):
                    for h in range(4):
                        ins = e.transpose(out=tbk[:, h * 128:(h + 1) * 128], in_=qtok[:, h * 128:(h + 1) * 128], identity=ident[:])
                    return ins
                op("pe", f_trq, r=["qtok", "ident"], w=[ktb])
                op("dve", (lambda tbk: (lambda e: e.tensor_copy(out=qT[:].rearrange("p h t -> p (h t)"), in_=tbk[:, 0:512])))(tbk), w=[ktb, "qT"])
                tbk, ktb = tbank()
                def f_trk(e, tbk=tbk):
                    for h in range(4):
                        ins = e.transpose(out=tbk[:, h * 128:(h + 1) * 128], in_=ktok[:, h * 128:(h + 1) * 128], identity=ident[:])
                    return ins
                op("pe", f_trk, r=["ktok", "ident"], w=[ktb])
                op("act", (lambda tbk, K: (lambda e: e.activation(out=K[:].rearrange("p h t -> p (h t)"), in_=tbk[:, 0:512], func=AF.Copy)))(tbk, kT[kslot]),
                   w=[ktb, "kT%d" % kslot])
                yield 0.5

                blocks = [j for j in range(5) if i - 4 + j >= 0]
                for g in range(2):
                    for j in blocks:
                        ti = i - 4 + j
                        ks = ti % NR
                        bk, kb = fbank()
                        def f_sc(e, bk=bk, ks=ks, g=g):
                            for hl in range(4):
                                h = 4 * g + hl
                                ins = e.matmul(bk[:, hl * 128:(hl + 1) * 128], lhsT=kT[ks][g * 64:(g + 1) * 64, hl, :],
                                               rhs=qT[g * 64:(g + 1) * 64, hl, :], start=True, stop=True)
                            return ins
                        op("pe", f_sc, r=["kT%d" % ks, "qT"], w=[kb])
                        if j <= 2:
                            op("act", (lambda bk, j: (lambda e: e.activation(out=PT[:, j, :, :], in_=bk[:, :].rearrange("p (h q) -> p h q", h=4), func=AF.Exp)))(bk, j),
                               w=[kb, "PT"])
                            if j == 0:
                                op("pool", lambda e: e.memset(PT[0:64, 0, :, 64:128], 0.0), w=["PT"])
                        else:
                            op("dve", (lambda bk, j, g: (lambda e: e.tensor_tensor(out=bk[:, :].rearrange("p (h q) -> p h q", h=4), in0=bk[:, :].rearrange("p (h q) -> p h q", h=4),
                                                                                in1=biasT[:, j - 3, :, :].rearrange("p (hl two) q -> p hl two q", two=2)[:, :, g, :], op=ALU.add)))(bk, j, g),
                               r=["biasT"], w=[kb])
                            op("act", (lambda bk, j: (lambda e: e.activation(out=PT[:, j, :, :], in_=bk[:, :].rearrange("p (h q) -> p h q", h=4), func=AF.Exp)))(bk, j), w=[kb, "PT"])
                        yield 0.25
                    bk, kb = fbank()
                    def f_pv(e, bk=bk, g=g, blocks=blocks, i=i):
                        for hl in range(4):
                            h = 2 * hl + g
                            for n, j in enumerate(blocks):
                                vs_ = (i - 4 + j) % NRV
                                ins = e.matmul(bk[:, hl * 65:(hl + 1) * 65], lhsT=PT[:, j, hl, :], rhs=vext[vs_][:, h, :],
                                               start=(n == 0), stop=(n == len(blocks) - 1))
                        return ins
                    op("pe", f_pv, r=["PT"] + ["vext%d" % ((i - 4 + j) % NRV) for j in blocks], w=[kb])
                    op("dve", (lambda bk: (lambda e: e.reciprocal(out=rcp[:], in_=bk[:, 0:260].rearrange("p (h d) -> p h d", h=4)[:, :, 64])))(bk), w=[kb, "rcp"])
                    op("dve", (lambda bk: (lambda e: e.tensor_tensor(out=attn_t[:].rearrange("p (h d) -> p h d", h=4),
                                                                    in0=bk[:, 0:260].rearrange("p (h d) -> p h d", h=4)[:, :, 0:64],
                                                                    in1=rcp[:, :].unsqueeze(2).to_broadcast([128, 4, 64]), op=ALU.mult)))(bk),
                       r=["rcp"], w=[kb, "attn_t"])
                    op("dve", (lambda g: (lambda e: e.tensor_tensor(out=cat[:, 0:512].rearrange("p (hl two d) -> p hl two d", two=2, d=64)[:, :, g, :],
                                                                    in0=attn_t[:].rearrange("p (hl d) -> p hl d", d=64),
                                                                    in1=sila[:, :].rearrange("p (hl two d) -> p hl two d", two=2, d=64)[:, :, g, :], op=ALU.mult)))(g),
                       r=["attn_t", ks_sila], w=["tokbf_a"])
                    yield 1.3


            def gen_C():
                bkz, kbz = fbank()
                op("pe", (lambda bk: (lambda e: e.matmul(bk[:, 0:256], lhsT=glrT[:, :], rhs=wgg[:, :], start=True, stop=True)))(bkz), r=[ks_glrT, "wgg"], w=[kbz])
                op("act", (lambda bk: (lambda e: e.activation(out=nla[:], in_=bk[:, 0:256], func=AF.Exp, scale=-1.0)))(bkz), w=[kbz, "nla"])
                op("act", lambda e: e.activation(out=nla[:], in_=nla[:], func=AF.Ln, bias=1.0, scale=1.0), w=["nla"])
                yield 0.6
                bkc, kbc = fbank()
                def f_cum(e, bk=bkc):
                    e.matmul(bk[:, 0:256], lhsT=Mle, rhs=nla[:, :], start=True, stop=True)
                    return e.matmul(bk[:, 256:512], lhsT=Mgt, rhs=nla[:, :], start=True, stop=True)
                op("pe", f_cum, r=["nla", "cst"], w=[kbc])
                bke, kbe = fbank()
                def f_cend(e, bk=bke):
                    for h in range(4):
                        ins = e.matmul(bk[0:64, 2 * h:2 * h + 2], lhsT=nla[:, h * 64:(h + 1) * 64], rhs=chunk_ind, start=True, stop=True)
                    return ins
                op("pe", f_cend, r=["nla", "cst"], w=[kbe])
                op("act", (lambda bk: (lambda e: e.activation(out=E1[:, 0:256], in_=bk[:, 0:256], func=AF.Exp, scale=-1.0 / 16)))(bkc), w=[kbc, "E1"])
                op("act", (lambda bk: (lambda e: e.activation(out=E1[:, 256:512], in_=bk[:, 0:256], func=AF.Exp, scale=1.0 / 16)))(bkc), w=[kbc, "E1"])
                op("act", (lambda bk: (lambda e: e.activation(out=eEnd[:], in_=bk[:, 256:512], func=AF.Exp, scale=-1.0 / 16)))(bkc), w=[kbc, "eEnd"])
                op("act", (lambda bk: (lambda e: e.activation(out=dec[:], in_=bk[0:64, 0:8], func=AF.Exp, scale=-1.0 / 16)))(bke), w=[kbe, "dec"])
                op("dve", lambda e: e.tensor_tensor(out=qfkn[:], in0=gqk[:], in1=E1[:], op=ALU.mult), r=[ks_gqk, "E1"], w=["qfkn"])
                op("dve", lambda e: e.tensor_tensor(out=qakp[:, 0:256], in0=gqk[:, 0:256], in1=E1[:, 256:512], op=ALU.mult), r=[ks_gqk, "E1"], w=["qakp"])
                op("dve", lambda e: e.tensor_tensor(out=qakp[:, 256:512], in0=gqk[:, 256:512], in1=E1[:, 0:256], op=ALU.mult), r=[ks_gqk, "E1"], w=["qakp"])
                op("pool", lambda e: e.tensor_tensor(out=ke[:], in0=gqk[:, 256:512], in1=eEnd[:], op=ALU.mult), r=[ks_gqk, "eEnd"], w=["ke"])
                yield 1.6
                tbk, ktb = tbank()
                def f_trg1(e, tbk=tbk):
                    for n in range(8):
                        ins = e.transpose(out=tbk[0:64, n * 128:(n + 1) * 128], in_=qfkn[:, n * 64:(n + 1) * 64], identity=ident[:])
                    return ins
                op("pe", f_trg1, r=["qfkn", "ident"], w=[ktb])
                op("act", (lambda tbk: (lambda e: e.activation(out=GT1[:].rearrange("p n t -> p (n t)"), in_=tbk[0:64, :], func=AF.Copy)))(tbk), w=[ktb, "GT1"])
                op("dve", (lambda tbk: (lambda e: e.tensor_copy(out=QA[0:64, :, 0:64], in_=tbk[0:64, 0:512].rearrange("p (h t) -> p h t", h=4)[:, :, 0:64])))(tbk), w=[ktb, "QA"])
                op("dve", (lambda tbk: (lambda e: e.tensor_copy(out=QB[0:64, :, 64:128], in_=tbk[0:64, 0:512].rearrange("p (h t) -> p h t", h=4)[:, :, 64:128])))(tbk), w=[ktb, "QB"])
                tbk, ktb = tbank()
                def f_trg2(e, tbk=tbk):
                    for n in range(8):
                        ins = e.transpose(out=tbk[0:64, n * 128:(n + 1) * 128], in_=qakp[:, n * 64:(n + 1) * 64], identity=ident[:])
                    return ins
                op("pe", f_trg2, r=["qakp", "ident"], w=[ktb])
                op("dve", (lambda tbk: (lambda e: e.tensor_copy(out=GT2[:].rearrange("p n t -> p (n t)"), in_=tbk[0:64, :])))(tbk), w=[ktb, "GT2"])
                yield 1.0
                bk1, kb1 = fbank()
                def f_ac(e, bk=bk1):
                    for h in range(4):
                        ins = e.matmul(bk[:, h * 128:(h + 1) * 128], lhsT=GT1[:, 4 + h, :], rhs=GT1[:, h, :], start=True, stop=True)
                    return ins
                op("pe", f_ac, r=["GT1"], w=[kb1])
                bk2, kb2 = fbank()
                def f_aa(e, bk=bk2):
                    for h in range(4):
                        ins = e.matmul(bk[:, h * 128:(h + 1) * 128], lhsT=GT2[:, 4 + h, :], rhs=GT2[:, h, :], start=True, stop=True)
                    return ins
                op("pe", f_aa, r=["GT2"], w=[kb2])
                op("dve", (lambda bk: (lambda e: e.tensor_tensor(out=attT[:], in0=bk[:, :].rearrange("p (h t) -> p h t", h=4),
                                                                in1=Mle.unsqueeze(1).to_broadcast([128, 4, 128]), op=ALU.mult)))(bk1), r=["cst"], w=[kb1, "attT"])
                op("dve", (lambda bk: (lambda e: e.tensor_tensor(out=att2, in0=bk[:, :].rearrange("p (h t) -> p h t", h=4),
                                                                in1=Mgt.unsqueeze(1).to_broadcast([128, 4, 128]), op=ALU.mult)))(bk2), r=["cst"], w=[kb2, "qfkn"])
                op("pool", lambda e: e.tensor_tensor(out=attT[:], in0=attT[:], in1=att2, op=ALU.add), r=["qfkn"], w=["attT"])
                yield 0.5
                for cc in range(2):
                    cidx = 2 * i + cc
                    bkk, kbk = fbank()
                    def f_kv(e, bk=bkk, cc=cc):
                        for h in range(4):
                            ins = e.matmul(bk[0:64, h * 128:(h + 1) * 128], lhsT=ke[cc * 64:(cc + 1) * 64, h * 64:(h + 1) * 64],
                                           rhs=vtok[cc * 64:(cc + 1) * 64, h * 128:(h + 1) * 128], start=True, stop=True)
                        return ins
                    op("pe", f_kv, r=["ke", ks_vtok], w=[kbk])
                    op("dve", (lambda cc: (lambda e: e.tensor_tensor(out=Sst[:], in0=Sst[:],
                                                                    in1=dec[:, :].rearrange("p (h c) -> p h c", c=2)[:, :, cc:cc + 1].to_broadcast([64, 4, 128]),
                                                                    op=ALU.mult)))(cc), r=["dec"], w=["Sst"])
                    op("dve", (lambda bk: (lambda e: e.tensor_tensor(out=Sst[:], in0=Sst[:], in1=bk[0:64, :].rearrange("p (h v) -> p h v", h=4), op=ALU.add)))(bkk),
                       w=[kbk, "Sst"])
                    nb = (cidx + 1) % 3
                    op("act", (lambda nb: (lambda e: e.activation(out=Sbf[nb][0:64, :, :], in_=Sst[:], func=AF.Copy)))(nb), r=["Sst"], w=["Sbf%d" % nb])
                    yield 0.3
                bko, kbo = fbank()
                s0, s1 = (2 * i) % 3, (2 * i + 1) % 3
                def f_o(e, bk=bko, s0=s0, s1=s1):
                    for h in range(4):
                        e.matmul(bk[:, h * 128:(h + 1) * 128], lhsT=attT[:, h, :], rhs=vtok[:, h * 128:(h + 1) * 128], start=True, stop=False)
                        e.matmul(bk[:, h * 128:(h + 1) * 128], lhsT=QA[:, h, :], rhs=Sbf[s0][:, h, :], start=False, stop=False)
                        ins = e.matmul(bk[:, h * 128:(h + 1) * 128], lhsT=QB[:, h, :], rhs=Sbf[s1][:, h, :], start=False, stop=True)
                    return ins
                op("pe", f_o, r=["attT", ks_vtok, "QA", "QB", "Sbf%d" % s0, "Sbf%d" % s1], w=[kbo])
                op("act", (lambda bk: (lambda e: e.activation(out=osb[:], in_=bk[:, :], func=AF.Copy)))(bko), w=[kbo, "E1"])
                def f_ssq(e):
                    for h in range(4):
                        ins = e.activation(out=sqj[:, h * 128:(h + 1) * 128], in_=osb[:, h * 128:(h + 1) * 128], func=AF.Square, accum_out=ssq[:, h:h + 1])
                    return ins
                op("act", f_ssq, r=["E1"], w=["qakp", "ssq"])
                op("act", lambda e: e.activation(out=rms[:], in_=ssq[:], func=AF.Ln, bias=RMS_EPS, scale=1.0 / 128), r=["ssq"], w=["rms"])
                op("act", lambda e: e.activation(out=rms[:], in_=rms[:], func=AF.Exp, scale=-0.5), w=["rms"])
                op("dve", lambda e: e.tensor_tensor(out=osb[:].rearrange("p (h v) -> p h v", h=4), in0=osb[:].rearrange("p (h v) -> p h v", h=4),
                                                    in1=rms[:, :].unsqueeze(2).to_broadcast([128, 4, 128]), op=ALU.mult), r=["rms"], w=["E1"])
                op("dve", lambda e: e.tensor_tensor(out=cat[:, 512:1024], in0=osb[:], in1=silg[:], op=ALU.mult), r=["E1", ks_silg], w=["tokbf_g"])
                yield 0.9


            def gen_D():
                tbk, ktb = tbank()
                op("pe", (lambda tbk: (lambda e: f_tr8(e, tbk=tbk, src=cat)))(tbk), r=["tokbf_a", "tokbf_g", "ident"], w=[ktb])
                op("act", (lambda tbk: (lambda e: e.activation(out=catT[:].rearrange("p c t -> p (c t)"), in_=tbk[:, :], func=AF.Copy)))(tbk), w=[ktb, "featT"])
                mixb = []
                for n in range(2):
                    bk, kb = fbank()
                    def f_mix(e, bk=bk, n=n):
                        for c in range(8):
                            ins = e.matmul(bk[:, :], lhsT=catT[:, c, :], rhs=wout[:, c, n * 512:(n + 1) * 512], start=(c == 0), stop=(c == 7))
                        return ins
                    op("pe", f_mix, r=["featT", KWOUT, KLATE["wout"]], w=[kb])
                    op("dve", (lambda bk, n, RR: (lambda e: e.scalar_tensor_tensor(out=RR[:, n * 512:(n + 1) * 512], in0=RR[:, n * 512:(n + 1) * 512], scalar=ALPHA,
                                                                                   in1=bk[:, :], op0=ALU.mult, op1=ALU.add)))(bk, n, RR), w=[kb, krr])
                    yield 1.7 + (0.6 if n == 0 else 0)
                op("dve", (lambda RR: (lambda e: e.tensor_copy(out=rbf[:], in_=RR[:])))(RR), r=[krr], w=["rbf"])
                tbk, ktb = tbank()
                op("pe", (lambda tbk: (lambda e: f_tr8(e, tbk=tbk, src=rbf)))(tbk), r=["rbf", "ident"], w=[ktb])
                op("dve", (lambda tbk: (lambda e: e.tensor_copy(out=featT[:].rearrange("p c t -> p (c t)"), in_=tbk[:, :])))(tbk), w=[ktb, "featT"])
                op("pool", (lambda PTt: (lambda e: e.tensor_copy(out=pbf[:], in_=PTt[:])))(PTt), r=[kp_], w=["pbf"])
                tbk, ktb = tbank()
                def f_trp(e, tbk=tbk):
                    for c in range(2):
                        ins = e.transpose(out=tbk[:, c * 128:(c + 1) * 128], in_=pbf[:, c * 128:(c + 1) * 128], identity=ident[:])
                    return ins
                op("pe", f_trp, r=["pbf", "ident"], w=[ktb])
                op("dve", (lambda tbk: (lambda e: e.tensor_copy(out=pT[:].rearrange("p c t -> p (c t)"), in_=tbk[:, 0:256])))(tbk), w=[ktb, "pT"])
                yield 0.8
                for n in range(2):
                    bkg, kbg = fbank()
                    def f_gate(e, bk=bkg, n=n):
                        for c in range(8):
                            e.matmul(bk[:, :], lhsT=featT[:, c, :], rhs=wgate[:, c, n * 512:(n + 1) * 512], start=(c == 0), stop=False)
                        e.matmul(bk[:, :], lhsT=ones_bf[0:1, :], rhs=bghi[0:1, n * 512:(n + 1) * 512], start=False, stop=False)
                        return e.matmul(bk[:, :], lhsT=ones_bf[0:1, :], rhs=bglo[0:1, n * 512:(n + 1) * 512], start=False, stop=True)
                    op("pe", f_gate, r=["featT", KLATE["wgate"], "ones_bf", "bghi", "bglo"], w=[kbg])
                    op("act", (lambda bk, n: (lambda e: e.activation(out=eg[:], in_=bk[:, :], func=AF.Exp, scale=-1.0)))(bkg, n), w=[kbg, "eg"])
                    op("act", lambda e: e.activation(out=eg[:], in_=eg[:], func=AF.Ln, bias=1.0, scale=1.0), w=["eg"])
                    op("act", lambda e: e.activation(out=eg[:], in_=eg[:], func=AF.Exp, scale=-1.0), w=["eg"])
                    bkp, kbp = fbank()
                    def f_ple(e, bk=bkp, n=n):
                        for c in range(2):
                            ins = e.matmul(bk[:, :], lhsT=pT[:, c, :], rhs=wple[:, c, n * 512:(n + 1) * 512], start=(c == 0), stop=(c == 1))
                        return ins
                    op("pe", f_ple, r=["pT", KLATE["wple"]], w=[kbp])
                    op("dve", (lambda bk: (lambda e: e.tensor_tensor(out=eg[:], in0=bk[:, :], in1=eg[:], op=ALU.mult)))(bkp), w=[kbp, "eg"])
                    op("pool", (lambda RR, n: (lambda e: e.tensor_tensor(out=RR[:, n * 512:(n + 1) * 512], in0=RR[:, n * 512:(n + 1) * 512], in1=eg[:], op=ALU.add)))(RR, n),
                       r=["eg"], w=[krr])
                    yield 2.6
                def f_stats2(e):
                    e.bn_stats(out=stats2[:, 0, :], in_=RR[:, 0:512])
                    return e.bn_stats(out=stats2[:, 1, :], in_=RR[:, 512:1024])
                op("dve", f_stats2, r=[krr], w=["stats2"])
                op("dve", lambda e: e.bn_aggr(out=mv2[:], in_=stats2[:]), r=["stats2"], w=["mv2"])
                op("dve", lambda e: e.tensor_scalar_mul(out=nmr2[:], in0=mv2[:, 0:1], scalar1=-1.0), r=["mv2"], w=["nmr2"])
                op("act", lambda e: e.activation(out=rstd2[:], in_=mv2[:, 1:2], func=AF.Ln, bias=LN_EPS, scale=1.0), r=["mv2"], w=["rstd2"])
                op("act", lambda e: e.activation(out=rstd2[:], in_=rstd2[:], func=AF.Exp, scale=-0.5), w=["rstd2"])
                op("act", lambda e: e.activation(out=nmr2[:], in_=nmr2[:], func=AF.Identity, scale=rstd2[:, 0:1]), r=["rstd2"], w=["nmr2"])
                op("act", lambda e: e.activation(out=RR[:], in_=RR[:], func=AF.Identity, scale=rstd2[:, 0:1], bias=nmr2[:, 0:1]), r=["rstd2", "nmr2"], w=[krr])
                op("pool", lambda e: e.tensor_tensor(out=RR[:], in0=RR[:], in1=vecs[:, 1, :], op=ALU.mult), r=["vecs"], w=[krr])
                op("pool", lambda e: e.dma_start(out=RR[:], in_=vec_d[3], accum_op=ALU.add), w=[krr], dma="acc" + krr)
                out_dmas.append(op("pool", lambda e: e.dma_start(out=out_d[i * 128:(i + 1) * 128, :], in_=RR[:]), r=[krr], dma="o" + krr))
                yield 0.5

            return dict(loads=loads, loads_p=loads_p, L=stage_L(), A1=gen_A1(), A2=gen_A2(), B=gen_B(), C=gen_C(), D=gen_D())

        tiles = [make_tile(i) for i in range(NT)]

        def step(g):
            try:
                return next(g)
            except StopIteration:
                return None

        def drain(g):
            while step(g) is not None:
                pass

        def merge(streams):
            for st_ in streams:
                st_["p"] = 0.0
                st_["done"] = False
            tau = 0.0
            while not all(st_["done"] for st_ in streams):
                best, lag = None, 0.0
                for st_ in streams:
                    if st_["done"]:
                        continue
                    if any(not d["done"] for d in st_.get("after", ())):
                        continue
                    tgt = min(1.0, max(0.0, (tau - st_["t0"]) / (st_["t1"] - st_["t0"])))
                    l = tgt - st_["p"] / st_["total"]
                    if l > lag:
                        best, lag = st_, l
                if best is None:
                    if tau > 1.5:
                        for st_ in streams:
                            if not st_["done"]:
                                drain(st_["g"])
                                st_["done"] = True
                        break
                    tau += 0.01
                    continue
                v = step(best["g"])
                if v is None:
                    best["done"] = True
                else:
                    best["p"] += v

        def chain(*gs):
            for g_ in gs:
                while True:
                    v = step(g_)
                    if v is None:
                        break
                    yield v

        for t in range(min(NX, NT)):
            tiles[t]["loads"]()
        drain(tiles[0]["L"])
        if NT > 1:
            for _ in range(4):
                step(tiles[1]["L"])
        step(tiles[0]["A1"])
        if NT > 1:
            drain(tiles[1]["L"])
        drain(tiles[0]["A1"])
        drain(tiles[0]["A2"])
        emit_late()
        for i in range(NT + 1):
            streams = []
            if i >= 1:
                tiles[i - 1]["loads_p"]()
                gD = tiles[i - 1]["D"]
                step(gD)
                streams.append(dict(g=gD, total=10.0, t0=SCHED["D"][0], t1=SCHED["D"][1]))
            if i == 0:
                streams.append(dict(g=gen_W(), total=8.0, t0=0.0, t1=0.9))
            if i < NT:
                gB, gC = tiles[i]["B"], tiles[i]["C"]
                step(gB)
                streams.append(dict(g=gB, total=5.1, t0=SCHED["B"][0], t1=SCHED["B"][1]))
                streams.append(dict(g=gC, total=5.2, t0=SCHED["C"][0], t1=SCHED["C"][1]))
                if i + 1 < NT:
                    gA = chain(tiles[i + 1]["A1"], tiles[i + 1]["A2"])
                    step(gA)
                    streams.append(dict(g=gA, total=12.5, t0=SCHED["A"][0], t1=SCHED["A"][1]))
                if i + 2 < NT:
                    streams.append(dict(g=tiles[i + 2]["L"], total=5.0, t0=SCHED["L"][0], t1=SCHED["L"][1]))
            merge(streams)
            if i >= 1 and i + NX - 1 < NT:
                tiles[i + NX - 1]["loads"]()
        P.emit(nc, final_waits=out_dmas)
    return nc


def _host_consts():
    a = np.arange(128)
    same = (a[:, None] // 64) == (a[None, :] // 64)
    ident = np.eye(128, dtype=np.float32)
    mle = (same & (a[:, None] <= a[None, :])).astype(np.float32)
    mgt = (same & (a[:, None] > a[None, :])).astype(np.float32)
    maskneg = np.where((a[:, None] >= 64) & (a[None, :] < 64), -1e30, 0.0).astype(np.float32)
    cind = np.zeros((128, 128), np.float32)
    cind[:64, 0] = 1.0
    cind[64:, 1] = 1.0
    return np.stack([mle, mgt, cind], 0), np.stack([ident, maskneg], 0)


def _prep_shared(inputs):
    f = np.float32
    vecs = np.stack([np.broadcast_to(np.asarray(inputs[k], f).reshape(1, D), (128, D))
                     for k in ("ln_in_g", "ln_in_b", "ln_g", "ln_b")], 0)
    gnorm = np.asarray(inputs["gla_norm_g"], f)[0].T
    rel = np.asarray(inputs["rel_bias"], f)[0]
    k = np.arange(128)[:, None]
    q = np.arange(128)[None, :]
    idx_prev = np.clip(q + 128 - k, -128, 128) + 128
    idx_diag = np.clip(q - k, -128, 128) + 128
    biasT = np.stack([rel[:, idx_prev], rel[:, idx_diag]], 0)
    biasT = np.ascontiguousarray(np.transpose(biasT, (2, 0, 1, 3)))
    cfar = np.broadcast_to(rel[:, 256].reshape(1, 8), (128, 8))
    wgg = np.concatenate([np.asarray(inputs["w_gla_gate"], f)[0], np.asarray(inputs["b_gla_gate"], f)[0].reshape(1, 256)], 0)
    return {
        "w_in": np.ascontiguousarray(np.asarray(inputs["w_in"], f)[0]),
        "w_out": np.ascontiguousarray(np.asarray(inputs["w_out"], f)[0]),
        "w_gate": np.ascontiguousarray(np.asarray(inputs["w_ple_gate"], f)[0]),
        "w_ple": np.ascontiguousarray(np.asarray(inputs["w_ple"], f)[0]),
        "vecs": np.ascontiguousarray(vecs),
        "gnorm": np.ascontiguousarray(gnorm),
        "bgate": np.ascontiguousarray(np.asarray(inputs["b_ple_gate"], f)[0].reshape(1, D)),
        "wgg": np.ascontiguousarray(wgg),
        "biasT": biasT,
        "cfar": np.ascontiguousarray(cfar),
        "consts": _host_consts()[0],
        "consts2": _host_consts()[1],
    }


def kernel(**inputs):
    x = np.asarray(inputs["x"], np.float32)
    p = np.asarray(inputs["p"], np.float32)[0]
    B, S, _ = x.shape
    NT = S // 128
    shared = _prep_shared(inputs)
    import os
    nc = build_nc(NT, dbg=os.environ.get("KDBG"))
    in_maps = []
    for b in range(B):
        m = dict(shared)
        m["x"] = np.ascontiguousarray(x[b])
        m["p"] = np.ascontiguousarray(p[b])
        in_maps.append(m)
    res = run_bass_kernel_spmd(nc, in_maps, core_ids=list(range(B)))
    out = np.stack([np.asarray(res.results[b]["out"], np.float32).reshape(S, D) for b in range(B)], 0)
    return out
```

```python
import contextlib
import numpy as np
import concourse.bass as bass
import concourse.mybir as mybir
from concourse.bass_utils import run_bass_kernel_spmd

F32 = mybir.dt.float32
BF16 = mybir.dt.bfloat16
AF = mybir.ActivationFunctionType
ALU = mybir.AluOpType

D = 1024
DP = 3600
NCORES = 8
ALPHA = 2.0 ** 0.25
LN_EPS = 1e-5
RMS_EPS = 1e-6

SCHED = {"D": (0.0, 0.67), "B": (0.0, 0.8), "C": (0.04, 0.7), "A": (0.1, 0.79), "L": (0.16, 0.45)}


class _Op:
    __slots__ = ("eng", "fn", "deps", "needed", "count", "dsem", "dcount", "is_dma")

    def __init__(self, eng, fn):
        self.eng = eng
        self.fn = fn
        self.deps = []
        self.needed = False
        self.count = None
        self.dsem = None
        self.dcount = None
        self.is_dma = False


class Prog:
    ENGS = ("pe", "act", "dve", "pool", "sp")

    def __init__(self):
        self.ops = {e: [] for e in self.ENGS}
        self.lastw = {}
        self.readers = {}
        self.dma_keys = {}

    def op(self, eng, fn, r=(), w=(), dma=None):
        o = _Op(eng, fn)
        deps = []
        seen = set()

        def add(d):
            if d is o or id(d) in seen:
                return
            if d.eng == "pe" and eng == "pe" and not d.is_dma and dma is None:
                return
            seen.add(id(d))
            deps.append(d)

        for k in r:
            d = self.lastw.get(k)
            if d is not None:
                add(d)
        for k in w:
            d = self.lastw.get(k)
            if d is not None:
                add(d)
            for rd in self.readers.get(k, ()):
                add(rd)
        o.deps = deps
        for d in deps:
            d.needed = True
        for k in r:
            self.readers.setdefault(k, []).append(o)
        for k in w:
            self.lastw[k] = o
            self.readers[k] = []
        if dma is not None:
            o.is_dma = True
            o.dsem = dma
            c = self.dma_keys.get(dma, 0) + 16
            self.dma_keys[dma] = c
            o.dcount = c
        self.ops[eng].append(o)
        return o

    def emit(self, nc, final_waits=()):
        with contextlib.ExitStack() as st:
            esem = {e: st.enter_context(nc.semaphore("s_" + e)) for e in self.ENGS}
            dsem = {k: st.enter_context(nc.semaphore("d_" + str(k))) for k in self.dma_keys}
            for e in self.ENGS:
                c = 0
                for o in self.ops[e]:
                    if not o.is_dma and o.needed:
                        c += 1
                        o.count = c
            block = st.enter_context(nc.Block())

            def signal(d):
                if d.is_dma:
                    return dsem[d.dsem], d.dcount
                return esem[d.eng], d.count

            def run(ename, engine):
                waited = {}
                for o in self.ops[ename]:
                    for d in o.deps:
                        s, v = signal(d)
                        if waited.get(id(s), 0) >= v:
                            continue
                        waited[id(s)] = v
                        engine.wait_ge(s, v)
                    ins = o.fn(engine)
                    if o.is_dma:
                        ins.then_inc(dsem[o.dsem], 16)
                    elif o.needed:
                        ins.then_inc(esem[ename], 1)
                if ename == "sp":
                    for d in final_waits:
                        s, v = signal(d)
                        if waited.get(id(s), 0) >= v:
                            continue
                        waited[id(s)] = v
                        engine.wait_ge(s, v)

            @block.tensor
            def _(e):
                run("pe", e)

            @block.scalar
            def _(e):
                run("act", e)

            @block.vector
            def _(e):
                run("dve", e)

            @block.gpsimd
            def _(e):
                run("pool", e)

            @block.sync
            def _(e):
                run("sp", e)


def build_nc(NT, dbg=None):
    nc = bass.Bass("TRN2", target_bir_lowering=False)
    T = NT * 128

    def din(name, shape):
        return nc.dram_tensor(name, list(shape), F32, kind="ExternalInput").ap()

    x_d = din("x", [T, D])
    p_d = din("p", [T, 256])
    win_d = din("w_in", [D, DP])
    wout_d = din("w_out", [D, D])
    wgate_d = din("w_gate", [D, D])
    wple_d = din("w_ple", [256, D])
    vec_d = din("vecs", [4, 128, D])
    gnorm_d = din("gnorm", [128, 4])
    bgate_d = din("bgate", [1, D])
    wgg_d = din("wgg", [17, 256])
    bias_d = din("biasT", [128, 2, 8, 128])
    cfar_d = din("cfar", [128, 8])
    cst_d = din("consts", [3, 128, 128])
    cst2_d = din("consts2", [2, 128, 128])
    out_d = nc.dram_tensor("out", [T, D], F32, kind="ExternalOutput").ap()

    st = contextlib.ExitStack()

    def sb(name, shape, dt=F32):
        return st.enter_context(nc.sbuf_tensor(name, list(shape), dt))

    with st:
        win = sb("win", [128, 8, DP], BF16)
        wout = sb("wout", [128, 8, D], BF16)
        wgate = sb("wgate", [128, 8, D], BF16)
        wple = sb("wple", [128, 2, D], BF16)
        vecs = sb("vecs_sb", [128, 2, D])
        gnorm = sb("gnorm_sb", [128, 4])
        bg32 = sb("bg32", [1, D])
        bghi = sb("bghi", [1, D], BF16)
        bglo = sb("bglo", [1, D], BF16)
        bgt = sb("bgt", [1, D])
        ones_bf = sb("ones_bf", [1, 128], BF16)
        wgg = sb("wgg_sb", [17, 256])
        biasT = sb("biasT_sb", [128, 2, 8, 128])
        cfar = sb("cfar_sb", [128, 8])
        cst = sb("cst_sb", [128, 3, 128])
        ident = sb("ident", [128, 128], BF16)

        NX = 4
        xt = [sb("xt%d" % i, [128, D]) for i in range(NX)]
        pt = [sb("pt%d" % i, [128, 256]) for i in range(1)]
        tokbf = sb("tokbf", [128, D], BF16)
        featT = sb("featT", [128, 8, 128], BF16)
        hbf = sb("hbf", [128, D], BF16)
        rbf = sb("rbf", [128, D], BF16)
        hT = sb("hT", [128, 8, 128], BF16)
        stats = sb("stats", [128, 2, 6])
        mv = sb("mv", [128, 2])
        rstd = sb("rstd", [128, 1])
        nmr = sb("nmr", [128, 1])
        nmr2 = sb("nmr2", [128, 1])
        qtok = sb("qtok", [128, 512], BF16)
        ktok = sb("ktok", [128, 512], BF16)
        qT = sb("qT", [128, 4, 128], BF16)
        NR = 5
        kT = [sb("kT%d" % i, [128, 4, 128], BF16) for i in range(NR)]
        NRV = 6
        vext = [sb("vext%d" % i, [128, 8, 65], BF16) for i in range(NRV)]
        sila2 = [sb("sila%d" % i, [128, 512]) for i in range(2)]
        silg2 = [sb("silg%d" % i, [128, 512]) for i in range(2)]
        gqk2 = [sb("gqk%d" % i, [128, 512]) for i in range(2)]
        vtok2 = [sb("vtok%d" % i, [128, 512], BF16) for i in range(2)]
        glrT2 = [sb("glrT%d" % i, [17, 128]) for i in range(2)]
        nla = sb("nla", [128, 256])
        E1 = sb("E1", [128, 512])
        eEnd = sb("eEnd", [128, 256])
        dec = sb("dec", [64, 8])
        qfkn = sb("qfkn", [128, 512], BF16)
        qakp = sb("qakp", [128, 512], BF16)
        ke = sb("ke", [128, 256], BF16)
        GT1 = sb("GT1", [64, 8, 128], BF16)
        GT2 = sb("GT2", [64, 8, 128], BF16)
        QA = sb("QA", [128, 4, 128], BF16)
        QB = sb("QB", [128, 4, 128], BF16)
        PT = sb("PT", [128, 5, 4, 128], BF16)
        rcp = sb("rcp", [128, 4])
        attn_t = sb("attn_t", [128, 256])
        attT = sb("attT", [128, 4, 128], BF16)
        Sst = sb("Sst", [64, 4, 128])
        Sbf = [sb("Sbf%d" % i, [128, 4, 128], BF16) for i in range(3)]
        osb = E1
        sqj = qakp
        att2 = qfkn[:].rearrange("p (h t) -> p h t", h=4)
        ssq = sb("ssq", [128, 4])
        rms = sb("rms", [128, 4])
        cat = tokbf
        catT = featT
        eg = sb("eg", [128, 512])
        pbf = sb("pbf", [128, 256], BF16)
        pT = sb("pT", [128, 2, 128], BF16)
        stats2 = sb("stats2", [128, 2, 6])
        mv2 = sb("mv2", [128, 2])
        rstd2 = sb("rstd2", [128, 1])

        NFB = 6
        fb = [st.enter_context(nc.psum_tensor("fb%d" % i, [128, 512], F32)) for i in range(NFB)]
        tb = [st.enter_context(nc.psum_tensor("tb%d" % i, [128, 1024], BF16)) for i in range(2)]
        fctr = [0]
        tctr = [0]

        def fbank():
            i = fctr[0] % NFB
            fctr[0] += 1
            return fb[i], "fb%d" % i

        def tbank():
            i = tctr[0] % 2
            tctr[0] += 1
            return tb[i], "tb%d" % i

        P = Prog()
        op = P.op

        op("sp", lambda e: e.dma_start(out=cst[:], in_=cst_d.rearrange("c p f -> p c f")), w=["cst"], dma="cst")
        op("sp", lambda e: e.dma_start(out=vecs[:, 0, :], in_=vec_d[0]), w=["vecs"], dma="vecs")
        op("sp", lambda e: e.dma_start(out=vecs[:, 1, :], in_=vec_d[2]), w=["vecs"], dma="vecs")
        op("sp", lambda e: e.dma_start(out=gnorm[:], in_=gnorm_d), w=["gnorm"], dma="gnorm")
        op("sp", lambda e: e.dma_start(out=bg32[:], in_=bgate_d), w=["bg32"], dma="bg32")
        op("sp", lambda e: e.dma_start(out=wgg[:], in_=wgg_d), w=["wgg"], dma="wgg")
        op("sp", lambda e: e.dma_start(out=biasT[:], in_=bias_d), w=["biasT"], dma="biasT")
        op("sp", lambda e: e.dma_start(out=cfar[:], in_=cfar_d), w=["cfar"], dma="cfar")
        op("sp", lambda e: e.dma_start(out=xt[2][:, 0:256].rearrange("p (c f) -> p c f", c=2), in_=cst2_d.rearrange("c p f -> p c f")), w=["xt2"], dma="xt2")
        op("dve", lambda e: e.tensor_copy(out=ident[:], in_=xt[2][:, 0:128]), r=["xt2"], w=["ident"])
        Mle = cst[:, 0, :]
        Mgt = cst[:, 1, :]
        maskneg = xt[2][:, 128:256]
        chunk_ind = cst[:, 2, 0:2]
        for b_ in range(2):
            op("dve", (lambda b_: (lambda e: e.tensor_tensor(
                out=biasT[:, b_, :, :], in0=biasT[:, b_, :, :],
                in1=cfar[:, :].unsqueeze(2).to_broadcast([128, 8, 128]), op=ALU.subtract)))(b_), r=["biasT", "cfar"], w=["biasT"])
        op("dve", lambda e: e.tensor_tensor(
            out=biasT[:, 1, :, :], in0=biasT[:, 1, :, :],
            in1=maskneg.unsqueeze(1).to_broadcast([128, 8, 128]), op=ALU.add), r=["biasT", "xt2"], w=["biasT"])
        op("dve", lambda e: e.tensor_copy(out=bghi[:], in_=bg32[:]), r=["bg32"], w=["bghi"])
        op("dve", lambda e: e.tensor_copy(out=bgt[:], in_=bghi[:]), r=["bghi"], w=["bgt"])
        op("dve", lambda e: e.tensor_tensor(out=bgt[:], in0=bg32[:], in1=bgt[:], op=ALU.subtract), r=["bg32", "bgt"], w=["bgt"])
        op("dve", lambda e: e.tensor_copy(out=bglo[:], in_=bgt[:]), r=["bgt"], w=["bglo"])
        op("dve", lambda e: e.memset(ones_bf[:], 1.0), w=["ones_bf"])
        for i in range(2):
            op("pool", (lambda t: (lambda e: e.memset(t[:], 1.0)))(glrT2[i]), w=["glrT%d" % i])
        op("pool", lambda e: e.memset(Sst[:], 0.0), w=["Sst"])
        for i in range(3):
            op("pool", (lambda t: (lambda e: e.memset(t[:], 0.0)))(Sbf[i]), w=["Sbf%d" % i])
        op("pool", lambda e: e.memset(QA[:], 0.0), w=["QA"])
        op("pool", lambda e: e.memset(QB[:], 0.0), w=["QB"])
        for i in range(NRV):
            op("pool", (lambda t: (lambda e: e.memset(t[:], 1.0)))(vext[i]), w=["vext%d" % i])

        cast_rr = [0]

        def load_w(dst, src_d, nchunk, ncols, wk, rowscale=None):
            for c in range(nchunk):
                c0 = 0
                while c0 < ncols:
                    cw = min(1024, ncols - c0)
                    s = cast_rr[0] % NX
                    eng = ("dve", "act")[cast_rr[0] % 2]
                    cast_rr[0] += 1
                    op("sp", (lambda s, c, c0, cw: (lambda e: e.dma_start(
                        out=xt[s][:, 0:cw], in_=src_d[c * 128:(c + 1) * 128, c0:c0 + cw])))(s, c, c0, cw),
                       w=["xt%d" % s], dma="xt%d" % s)
                    if rowscale is not None and c in rowscale:
                        eng = "dve"
                        f = (lambda s, c, c0, cw, sc: (lambda e: e.tensor_scalar_mul(out=dst[:, c, c0:c0 + cw], in0=xt[s][:, 0:cw], scalar1=sc)))(s, c, c0, cw, rowscale[c])
                    elif eng == "act":
                        f = (lambda s, c, c0, cw: (lambda e: e.activation(out=dst[:, c, c0:c0 + cw], in_=xt[s][:, 0:cw], func=AF.Copy)))(s, c, c0, cw)
                    else:
                        f = (lambda s, c, c0, cw: (lambda e: e.tensor_copy(out=dst[:, c, c0:c0 + cw], in_=xt[s][:, 0:cw])))(s, c, c0, cw)
                    op(eng, f, r=["xt%d" % s, "gnorm"], w=[wk])
                    c0 += cw

        KWIN, KWOUT, KWGATE, KWPLE = "win", "wout", "wgate", "wple"
        KWIN_KEYS = []
        for c in range(8):
            for c0 in (0, 1200, 2400):
                k_ = "win_%d_%d" % (c, c0)
                KWIN_KEYS.append(k_)
                op("pool", (lambda c, c0: (lambda e: e.dma_start(out=win[:, c, c0:c0 + 1200], in_=win_d[c * 128:(c + 1) * 128, c0:c0 + 1200])))(c, c0),
                   w=[k_], dma="dwin")

        KLATE = {"wout": "wout_3", "wgate": "wgate_7", "wple": "wple_1"}

        def emit_late():
            for dst, src_d, chunks, nm_ in ((wout, wout_d, range(4), "wout"), (wgate, wgate_d, range(8), "wgate"), (wple, wple_d, range(2), "wple")):
                for c in chunks:
                    k_ = "%s_%d" % (nm_, c)
                    op("pool", (lambda dst, src_d, c: (lambda e: e.dma_start(out=dst[:, c, :], in_=src_d[c * 128:(c + 1) * 128, :])))(dst, src_d, c),
                       r=[KWIN_KEYS[-1]], w=[k_], dma="d" + nm_)

        def gen_W():
            stg = [(eg[:, :], "eg"), (rbf[:, :].bitcast(F32), "rbf"), (featT[:].rearrange("p c t -> p (c t)").bitcast(F32), "featT")]
            rowscale = {4 + h: gnorm[:, h:h + 1] for h in range(4)}
            n = 0
            for dst, src_d, nchunk, wk in ((wout, wout_d, 8, KWOUT),):
                for c in range(4, nchunk):
                    for c0 in (0, 512):
                        sap, sk = stg[n % 3]
                        eng = ("dve", "act")[n % 2]
                        n += 1
                        op("sp", (lambda sap, src_d, c, c0: (lambda e: e.dma_start(out=sap, in_=src_d[c * 128:(c + 1) * 128, c0:c0 + 512])))(sap, src_d, c, c0),
                           w=[sk], dma="w" + sk)
                        if wk == KWOUT and c in rowscale:
                            f = (lambda sap, dst, c, c0, sc: (lambda e: e.tensor_scalar_mul(out=dst[:, c, c0:c0 + 512], in0=sap, scalar1=sc)))(sap, dst, c, c0, rowscale[c])
                            eng = "dve"
                        elif eng == "act":
                            f = (lambda sap, dst, c, c0: (lambda e: e.activation(out=dst[:, c, c0:c0 + 512], in_=sap, func=AF.Copy)))(sap, dst, c, c0)
                        else:
                            f = (lambda sap, dst, c, c0: (lambda e: e.tensor_copy(out=dst[:, c, c0:c0 + 512], in_=sap)))(sap, dst, c, c0)
                        op(eng, f, r=[sk, "gnorm"], w=[wk])
                        yield 1.0

        out_dmas = []

        def make_tile(i):
            xs = i % NX
            kslot = i % NR
            XT, PTt = xt[xs], pt[0]
            kx, kp_ = "xt%d" % xs, "pt0"
            hb = i % 2
            sila, silg, gqk, vtok, glrT = sila2[hb], silg2[hb], gqk2[hb], vtok2[hb], glrT2[hb]
            ks_sila, ks_silg, ks_gqk, ks_vtok, ks_glrT = "sila%d" % hb, "silg%d" % hb, "gqk%d" % hb, "vtok%d" % hb, "glrT%d" % hb
            vslot = i % NRV
            RR = XT
            krr = kx
            hf = XT

            def f_tr8(e, tbk=None, src=hbf):
                for c in range(8):
                    ins = e.transpose(out=tbk[:, c * 128:(c + 1) * 128], in_=src[:, c * 128:(c + 1) * 128], identity=ident[:])
                return ins
            def inproj(n0, ncols):
                bk, kb = fbank()
                def f(e, bk=bk):
                    for c in range(8):
                        ins = e.matmul(bk[:, 0:ncols], lhsT=hT[:, c, :], rhs=win[:, c, n0:n0 + ncols], start=(c == 0), stop=(c == 7))
                    return ins
                op("pe", f, r=["hT", KWIN_KEYS[-1]], w=[kb])
                return bk, kb

            def loads():
                op("sp", lambda e: e.dma_start(out=XT[:], in_=x_d[i * 128:(i + 1) * 128, :]), w=[kx], dma=kx)

            def loads_p():
                op("sp", lambda e: e.dma_start(out=PTt[:], in_=p_d[i * 128:(i + 1) * 128, :]), w=[kp_], dma=kp_)

            def stage_L():
                def f_stats(e):
                    e.bn_stats(out=stats[:, 0, :], in_=XT[:, 0:512])
                    return e.bn_stats(out=stats[:, 1, :], in_=XT[:, 512:1024])
                op("dve", f_stats, r=[kx], w=["stats"])
                op("dve", lambda e: e.bn_aggr(out=mv[:], in_=stats[:]), r=["stats"], w=["mv"])
                op("dve", lambda e: e.tensor_scalar_mul(out=nmr[:], in0=mv[:, 0:1], scalar1=-1.0), r=["mv"], w=["nmr"])
                yield 1.0
                op("act", lambda e: e.activation(out=rstd[:], in_=mv[:, 1:2], func=AF.Ln, bias=LN_EPS, scale=1.0), r=["mv"], w=["rstd"])
                op("act", lambda e: e.activation(out=rstd[:], in_=rstd[:], func=AF.Exp, scale=-0.5), w=["rstd"])
                op("act", lambda e: e.activation(out=nmr[:], in_=nmr[:], func=AF.Identity, scale=rstd[:, 0:1]), r=["rstd"], w=["nmr"])
                op("act", lambda e: e.activation(out=XT[:], in_=XT[:], func=AF.Identity, scale=rstd[:, 0:1], bias=nmr[:, 0:1]), r=["rstd", "nmr"], w=[kx])
                yield 1.0
                op("pool", lambda e: e.tensor_tensor(out=XT[:], in0=XT[:], in1=vecs[:, 0, :], op=ALU.mult), r=["vecs"], w=[kx])
                yield 1.0
                op("pool", lambda e: e.dma_start(out=XT[:], in_=vec_d[1], accum_op=ALU.add), w=[kx], dma="acc" + kx)
                yield 1.5
                op("pool", lambda e: e.tensor_copy(out=hbf[:], in_=XT[:]), r=[kx], w=["hbf"])
                yield 0.5

            def gen_A1():
                tbk, ktb = tbank()
                op("pe", (lambda tbk: (lambda e: f_tr8(e, tbk=tbk, src=hbf)))(tbk), r=["hbf", "ident"], w=[ktb])
                op("act", (lambda tbk: (lambda e: e.activation(out=hT[:].rearrange("p c t -> p (c t)"), in_=tbk[:, :], func=AF.Copy)))(tbk), r=[], w=[ktb, "hT"])
                yield 0.6

                bk, kb = inproj(0, 512)
                op("act", (lambda bk: (lambda e: e.activation(out=qtok[:], in_=bk[:, :], func=AF.Identity, scale=0.125)))(bk), w=[kb, "qtok"])
                yield 1.7
                bk, kb = inproj(512, 512)
                op("dve", (lambda bk: (lambda e: e.tensor_copy(out=ktok[:], in_=bk[:, :])))(bk), w=[kb, "ktok"])
                yield 1.7

            def gen_A2():
                bk, kb = inproj(1024, 512)
                op("act", (lambda bk, V: (lambda e: e.activation(out=V[:, :, 0:64], in_=bk[:, :].rearrange("p (h d) -> p h d", h=8), func=AF.Copy)))(bk, vext[vslot]),
                   w=[kb, "vext%d" % vslot])
                yield 1.7
                bk, kb = inproj(1536, 512)
                op("act", (lambda bk: (lambda e: e.activation(out=sila[:], in_=bk[:, :], func=AF.Exp, scale=-1.0)))(bk), w=[kb, ks_sila])
                op("act", lambda e: e.activation(out=sila[:], in_=sila[:], func=AF.Ln, bias=1.0, scale=1.0), w=[ks_sila])
                op("act", lambda e: e.activation(out=sila[:], in_=sila[:], func=AF.Exp, scale=-1.0), w=[ks_sila])
                op("dve", (lambda bk: (lambda e: e.tensor_tensor(out=sila[:], in0=bk[:, :], in1=sila[:], op=ALU.mult)))(bk), w=[kb, ks_sila])
                yield 1.7
                bk, kb = inproj(2048, 512)
                op("act", (lambda bk: (lambda e: e.activation(out=gqk[:, 0:256], in_=bk[:, 0:256], func=AF.Identity, scale=0.125)))(bk), w=[kb, ks_gqk])
                op("act", (lambda bk: (lambda e: e.activation(out=gqk[:, 256:512], in_=bk[:, 256:512], func=AF.Copy)))(bk), w=[kb, ks_gqk])
                yield 1.7
                bk, kb = inproj(2560, 512)
                op("dve", (lambda bk: (lambda e: e.tensor_copy(out=vtok[:], in_=bk[:, :])))(bk), w=[kb, ks_vtok])
                yield 1.7
                bk, kb = inproj(3072, 512)
                op("act", (lambda bk: (lambda e: e.activation(out=silg[:], in_=bk[:, :], func=AF.Exp, scale=-1.0)))(bk), w=[kb, ks_silg])
                op("act", lambda e: e.activation(out=silg[:], in_=silg[:], func=AF.Ln, bias=1.0, scale=1.0), w=[ks_silg])
                op("act", lambda e: e.activation(out=silg[:], in_=silg[:], func=AF.Exp, scale=-1.0), w=[ks_silg])
                op("dve", (lambda bk: (lambda e: e.tensor_tensor(out=silg[:], in0=bk[:, :], in1=silg[:], op=ALU.mult)))(bk), w=[kb, ks_silg])
                yield 1.7
                bk, kb = fbank()
                def f_glr(e, bk=bk):
                    for c in range(8):
                        ins = e.matmul(bk[0:16, 0:128], lhsT=win[:, c, 3584:3600], rhs=hT[:, c, :], start=(c == 0), stop=(c == 7))
                    return ins
                op("pe", f_glr, r=["hT", KWIN_KEYS[-1]], w=[kb])
                op("act", (lambda bk: (lambda e: e.activation(out=glrT[0:16, :], in_=bk[0:16, 0:128], func=AF.Copy)))(bk), w=[kb, ks_glrT])
                yield 0.6

            def gen_B():
                tbk, ktb = tbank()
                def f_trq(e, tbk=tbk):
                    for h in range(4):
                        ins = e.transpose(out=tbk[:, h * 128:(h + 1) * 128], in_=qtok[:, h * 128:(h + 1) * 128], identity=ident[:])
                    return ins
                op("pe", f_trq, r=["qtok", "ident"], w=[ktb])
                op("dve", (lambda tbk: (lambda e: e.tensor_copy(out=qT[:].rearrange("p h t -> p (h t)"), in_=tbk[:, 0:512])))(tbk), w=[ktb, "qT"])
                tbk, ktb = tbank()
                def f_trk(e, tbk=tbk):
                    for h in range(4):
                        ins = e.transpose(out=tbk[:, h * 128:(h + 1) * 128], in_=ktok[:, h * 128:(h + 1) * 128], identity=ident[:])
                    return ins
                op("pe", f_trk, r=["ktok", "ident"], w=[ktb])
                op("act", (lambda tbk, K: (lambda e: e.activation(out=K[:].rearrange("p h t -> p (h t)"), in_=tbk[:, 0:512], func=AF.Copy)))(tbk, kT[kslot]),
                   w=[ktb, "kT%d" % kslot])
                yield 0.5

                blocks = [j for j in range(5) if i - 4 + j >= 0]
                for g in range(2):
                    for j in blocks:
                        ti = i - 4 + j
                        ks = ti % NR
                        bk, kb = fbank()
                        def f_sc(e, bk=bk, ks=ks, g=g):
                            for hl in range(4):
                                h = 4 * g + hl
                                ins = e.matmul(bk[:, hl * 128:(hl + 1) * 128], lhsT=kT[ks][g * 64:(g + 1) * 64, hl, :],
                                               rhs=qT[g * 64:(g + 1) * 64, hl, :], start=True, stop=True)
                            return ins
                        op("pe", f_sc, r=["kT%d" % ks, "qT"], w=[kb])
                        if j <= 2:
                            op("act", (lambda bk, j: (lambda e: e.activation(out=PT[:, j, :, :], in_=bk[:, :].rearrange("p (h q) -> p h q", h=4), func=AF.Exp)))(bk, j),
                               w=[kb, "PT"])
                            if j == 0:
                                op("pool", lambda e: e.memset(PT[0:64, 0, :, 64:128], 0.0), w=["PT"])
                        else:
                            op("dve", (lambda bk, j, g: (lambda e: e.tensor_tensor(out=bk[:, :].rearrange("p (h q) -> p h q", h=4), in0=bk[:, :].rearrange("p (h q) -> p h q", h=4),
                                                                                in1=biasT[:, j - 3, :, :].rearrange("p (hl two) q -> p hl two q", two=2)[:, :, g, :], op=ALU.add)))(bk, j, g),
                               r=["biasT"], w=[kb])
                            op("act", (lambda bk, j: (lambda e: e.activation(out=PT[:, j, :, :], in_=bk[:, :].rearrange("p (h q) -> p h q", h=4), func=AF.Exp)))(bk, j), w=[kb, "PT"])
                        yield 0.25
                    bk, kb = fbank()
                    def f_pv(e, bk=bk, g=g, blocks=blocks, i=i):
                        for hl in range(4):
                            h = 2 * hl + g
                            for n, j in enumerate(blocks):
                                vs_ = (i - 4 + j) % NRV
                                ins = e.matmul(bk[:, hl * 65:(hl + 1) * 65], lhsT=PT[:, j, hl, :], rhs=vext[vs_][:, h, :],
                                               start=(n == 0), stop=(n == len(blocks) - 1))
                        return ins
                    op("pe", f_pv, r=["PT"] + ["vext%d" % ((i - 4 + j) % NRV) for j in blocks], w=[kb])
                    op("dve", (lambda bk: (lambda e: e.reciprocal(out=rcp[:], in_=bk[:, 0:260].rearrange("p (h d) -> p h d", h=4)[:, :, 64])))(bk), w=[kb, "rcp"])
                    op("dve", (lambda bk: (lambda e: e.tensor_tensor(out=attn_t[:].rearrange("p (h d) -> p h d", h=4),
                                                                    in0=bk[:, 0:260].rearrange("p (h d) -> p h d", h=4)[:, :, 0:64],
                                                                    in1=rcp[:, :].unsqueeze(2).to_broadcast([128, 4, 64]), op=ALU.mult)))(bk),
                       r=["rcp"], w=[kb, "attn_t"])
                    op("dve", (lambda g: (lambda e: e.tensor_tensor(out=cat[:, 0:512].rearrange("p (hl two d) -> p hl two d", two=2, d=64)[:, :, g, :],
                                                                    in0=attn_t[:].rearrange("p (hl d) -> p hl d", d=64),
                                                                    in1=sila[:, :].rearrange("p (hl two d) -> p hl two d", two=2, d=64)[:, :, g, :], op=ALU.mult)))(g),
                       r=["attn_t", ks_sila], w=["tokbf_a"])
                    yield 1.3


            def gen_C():
                bkz, kbz = fbank()
                op("pe", (lambda bk: (lambda e: e.matmul(bk[:, 0:256], lhsT=glrT[:, :], rhs=wgg[:, :], start=True, stop=True)))(bkz), r=[ks_glrT, "wgg"], w=[kbz])
                op("act", (lambda bk: (lambda e: e.activation(out=nla[:], in_=bk[:, 0:256], func=AF.Exp, scale=-1.0)))(bkz), w=[kbz, "nla"])
                op("act", lambda e: e.activation(out=nla[:], in_=nla[:], func=AF.Ln, bias=1.0, scale=1.0), w=["nla"])
                yield 0.6
                bkc, kbc = fbank()
                def f_cum(e, bk=bkc):
                    e.matmul(bk[:, 0:256], lhsT=Mle, rhs=nla[:, :], start=True, stop=True)
                    return e.matmul(bk[:, 256:512], lhsT=Mgt, rhs=nla[:, :], start=True, stop=True)
                op("pe", f_cum, r=["nla", "cst"], w=[kbc])
                bke, kbe = fbank()
                def f_cend(e, bk=bke):
                    for h in range(4):
                        ins = e.matmul(bk[0:64, 2 * h:2 * h + 2], lhsT=nla[:, h * 64:(h + 1) * 64], rhs=chunk_ind, start=True, stop=True)
                    return ins
                op("pe", f_cend, r=["nla", "cst"], w=[kbe])
                op("act", (lambda bk: (lambda e: e.activation(out=E1[:, 0:256], in_=bk[:, 0:256], func=AF.Exp, scale=-1.0 / 16)))(bkc), w=[kbc, "E1"])
                op("act", (lambda bk: (lambda e: e.activation(out=E1[:, 256:512], in_=bk[:, 0:256], func=AF.Exp, scale=1.0 / 16)))(bkc), w=[kbc, "E1"])
                op("act", (lambda bk: (lambda e: e.activation(out=eEnd[:], in_=bk[:, 256:512], func=AF.Exp, scale=-1.0 / 16)))(bkc), w=[kbc, "eEnd"])
                op("act", (lambda bk: (lambda e: e.activation(out=dec[:], in_=bk[0:64, 0:8], func=AF.Exp, scale=-1.0 / 16)))(bke), w=[kbe, "dec"])
                op("dve", lambda e: e.tensor_tensor(out=qfkn[:], in0=gqk[:], in1=E1[:], op=ALU.mult), r=[ks_gqk, "E1"], w=["qfkn"])
                op("dve", lambda e: e.tensor_tensor(out=qakp[:, 0:256], in0=gqk[:, 0:256], in1=E1[:, 256:512], op=ALU.mult), r=[ks_gqk, "E1"], w=["qakp"])
                op("dve", lambda e: e.tensor_tensor(out=qakp[:, 256:512], in0=gqk[:, 256:512], in1=E1[:, 0:256], op=ALU.mult), r=[ks_gqk, "E1"], w=["qakp"])
                op("pool", lambda e: e.tensor_tensor(out=ke[:], in0=gqk[:, 256:512], in1=eEnd[:], op=ALU.mult), r=[ks_gqk, "eEnd"], w=["ke"])
                yield 1.6
                tbk, ktb = tbank()
                def f_trg1(e, tbk=tbk):
                    for n in range(8):
                        ins = e.transpose(out=tbk[0:64, n * 128:(n + 1) * 128], in_=qfkn[:, n * 64:(n + 1) * 64], identity=ident[:])
                    return ins
                op("pe", f_trg1, r=["qfkn", "ident"], w=[ktb])
                op("act", (lambda tbk: (lambda e: e.activation(out=GT1[:].rearrange("p n t -> p (n t)"), in_=tbk[0:64, :], func=AF.Copy)))(tbk), w=[ktb, "GT1"])
                op("dve", (lambda tbk: (lambda e: e.tensor_copy(out=QA[0:64, :, 0:64], in_=tbk[0:64, 0:512].rearrange("p (h t) -> p h t", h=4)[:, :, 0:64])))(tbk), w=[ktb, "QA"])
                op("dve", (lambda tbk: (lambda e: e.tensor_copy(out=QB[0:64, :, 64:128], in_=tbk[0:64, 0:512].rearrange("p (h t) -> p h t", h=4)[:, :, 64:128])))(tbk), w=[ktb, "QB"])
                tbk, ktb = tbank()
                def f_trg2(e, tbk=tbk):
                    for n in range(8):
                        ins = e.transpose(out=tbk[0:64, n * 128:(n + 1) * 128], in_=qakp[:, n * 64:(n + 1) * 64], identity=ident[:])
                    return ins
                op("pe", f_trg2, r=["qakp", "ident"], w=[ktb])
                op("dve", (lambda tbk: (lambda e: e.tensor_copy(out=GT2[:].rearrange("p n t -> p (n t)"), in_=tbk[0:64, :])))(tbk), w=[ktb, "GT2"])
                yield 1.0
                bk1, kb1 = fbank()
                def f_ac(e, bk=bk1):
                    for h in range(4):
                        ins = e.matmul(bk[:, h * 128:(h + 1) * 128], lhsT=GT1[:, 4 + h, :], rhs=GT1[:, h, :], start=True, stop=True)
                    return ins
                op("pe", f_ac, r=["GT1"], w=[kb1])
                bk2, kb2 = fbank()
                def f_aa(e, bk=bk2):
                    for h in range(4):
                        ins = e.matmul(bk[:, h * 128:(h + 1) * 128], lhsT=GT2[:, 4 + h, :], rhs=GT2[:, h, :], start=True, stop=True)
                    return ins
                op("pe", f_aa, r=["GT2"], w=[kb2])
                op("dve", (lambda bk: (lambda e: e.tensor_tensor(out=attT[:], in0=bk[:, :].rearrange("p (h t) -> p h t", h=4),
                                                                in1=Mle.unsqueeze(1).to_broadcast([128, 4, 128]), op=ALU.mult)))(bk1), r=["cst"], w=[kb1, "attT"])
                op("dve", (lambda bk: (lambda e: e.tensor_tensor(out=att2, in0=bk[:, :].rearrange("p (h t) -> p h t", h=4),
                                                                in1=Mgt.unsqueeze(1).to_broadcast([128, 4, 128]), op=ALU.mult)))(bk2), r=["cst"], w=[kb2, "qfkn"])
                op("pool", lambda e: e.tensor_tensor(out=attT[:], in0=attT[:], in1=att2, op=ALU.add), r=["qfkn"], w=["attT"])
                yield 0.5
                for cc in range(2):
                    cidx = 2 * i + cc
                    bkk, kbk = fbank()
                    def f_kv(e, bk=bkk, cc=cc):
                        for h in range(4):
                            ins = e.matmul(bk[0:64, h * 128:(h + 1) * 128], lhsT=ke[cc * 64:(cc + 1) * 64, h * 64:(h + 1) * 64],
                                           rhs=vtok[cc * 64:(cc + 1) * 64, h * 128:(h + 1) * 128], start=True, stop=True)
                        return ins
                    op("pe", f_kv, r=["ke", ks_vtok], w=[kbk])
                    op("dve", (lambda cc: (lambda e: e.tensor_tensor(out=Sst[:], in0=Sst[:],
                                                                    in1=dec[:, :].rearrange("p (h c) -> p h c", c=2)[:, :, cc:cc + 1].to_broadcast([64, 4, 128]),
                                                                    op=ALU.mult)))(cc), r=["dec"], w=["Sst"])
                    op("dve", (lambda bk: (lambda e: e.tensor_tensor(out=Sst[:], in0=Sst[:], in1=bk[0:64, :].rearrange("p (h v) -> p h v", h=4), op=ALU.add)))(bkk),
                       w=[kbk, "Sst"])
                    nb = (cidx + 1) % 3
                    op("act", (lambda nb: (lambda e: e.activation(out=Sbf[nb][0:64, :, :], in_=Sst[:], func=AF.Copy)))(nb), r=["Sst"], w=["Sbf%d" % nb])
                    yield 0.3
                bko, kbo = fbank()
                s0, s1 = (2 * i) % 3, (2 * i + 1) % 3
                def f_o(e, bk=bko, s0=s0, s1=s1):
                    for h in range(4):
                        e.matmul(bk[:, h * 128:(h + 1) * 128], lhsT=attT[:, h, :], rhs=vtok[:, h * 128:(h + 1) * 128], start=True, stop=False)
                        e.matmul(bk[:, h * 128:(h + 1) * 128], lhsT=QA[:, h, :], rhs=Sbf[s0][:, h, :], start=False, stop=False)
                        ins = e.matmul(bk[:, h * 128:(h + 1) * 128], lhsT=QB[:, h, :], rhs=Sbf[s1][:, h, :], start=False, stop=True)
                    return ins
                op("pe", f_o, r=["attT", ks_vtok, "QA", "QB", "Sbf%d" % s0, "Sbf%d" % s1], w=[kbo])
                op("act", (lambda bk: (lambda e: e.activation(out=osb[:], in_=bk[:, :], func=AF.Copy)))(bko), w=[kbo, "E1"])
                def f_ssq(e):
                    for h in range(4):
                        ins = e.activation(out=sqj[:, h * 128:(h + 1) * 128], in_=osb[:, h * 128:(h + 1) * 128], func=AF.Square, accum_out=ssq[:, h:h + 1])
                    return ins
                op("act", f_ssq, r=["E1"], w=["qakp", "ssq"])
                op("act", lambda e: e.activation(out=rms[:], in_=ssq[:], func=AF.Ln, bias=RMS_EPS, scale=1.0 / 128), r=["ssq"], w=["rms"])
                op("act", lambda e: e.activation(out=rms[:], in_=rms[:], func=AF.Exp, scale=-0.5), w=["rms"])
                op("dve", lambda e: e.tensor_tensor(out=osb[:].rearrange("p (h v) -> p h v", h=4), in0=osb[:].rearrange("p (h v) -> p h v", h=4),
                                                    in1=rms[:, :].unsqueeze(2).to_broadcast([128, 4, 128]), op=ALU.mult), r=["rms"], w=["E1"])
                op("dve", lambda e: e.tensor_tensor(out=cat[:, 512:1024], in0=osb[:], in1=silg[:], op=ALU.mult), r=["E1", ks_silg], w=["tokbf_g"])
                yield 0.9


            def gen_D():
                tbk, ktb = tbank()
                op("pe", (lambda tbk: (lambda e: f_tr8(e, tbk=tbk, src=cat)))(tbk), r=["tokbf_a", "tokbf_g", "ident"], w=[ktb])
                op("act", (lambda tbk: (lambda e: e.activation(out=catT[:].rearrange("p c t -> p (c t)"), in_=tbk[:, :], func=AF.Copy)))(tbk), w=[ktb, "featT"])
                mixb = []
                for n in range(2):
                    bk, kb = fbank()
                    def f_mix(e, bk=bk, n=n):
                        for c in range(8):
                            ins = e.matmul(bk[:, :], lhsT=catT[:, c, :], rhs=wout[:, c, n * 512:(n + 1) * 512], start=(c == 0), stop=(c == 7))
                        return ins
                    op("pe", f_mix, r=["featT", KWOUT, KLATE["wout"]], w=[kb])
                    op("dve", (lambda bk, n, RR: (lambda e: e.scalar_tensor_tensor(out=RR[:, n * 512:(n + 1) * 512], in0=RR[:, n * 512:(n + 1) * 512], scalar=ALPHA,
                                                                                   in1=bk[:, :], op0=ALU.mult, op1=ALU.add)))(bk, n, RR), w=[kb, krr])
                    yield 1.7 + (0.6 if n == 0 else 0)
                op("dve", (lambda RR: (lambda e: e.tensor_copy(out=rbf[:], in_=RR[:])))(RR), r=[krr], w=["rbf"])
                tbk, ktb = tbank()
                op("pe", (lambda tbk: (lambda e: f_tr8(e, tbk=tbk, src=rbf)))(tbk), r=["rbf", "ident"], w=[ktb])
                op("dve", (lambda tbk: (lambda e: e.tensor_copy(out=featT[:].rearrange("p c t -> p (c t)"), in_=tbk[:, :])))(tbk), w=[ktb, "featT"])
                op("pool", (lambda PTt: (lambda e: e.tensor_copy(out=pbf[:], in_=PTt[:])))(PTt), r=[kp_], w=["pbf"])
                tbk, ktb = tbank()
                def f_trp(e, tbk=tbk):
                    for c in range(2):
                        ins = e.transpose(out=tbk[:, c * 128:(c + 1) * 128], in_=pbf[:, c * 128:(c + 1) * 128], identity=ident[:])
                    return ins
                op("pe", f_trp, r=["pbf", "ident"], w=[ktb])
                op("dve", (lambda tbk: (lambda e: e.tensor_copy(out=pT[:].rearrange("p c t -> p (c t)"), in_=tbk[:, 0:256])))(tbk), w=[ktb, "pT"])
                yield 0.8
                for n in range(2):
                    bkg, kbg = fbank()
                    def f_gate(e, bk=bkg, n=n):
                        for c in range(8):
                            e.matmul(bk[:, :], lhsT=featT[:, c, :], rhs=wgate[:, c, n * 512:(n + 1) * 512], start=(c == 0), stop=False)
                        e.matmul(bk[:, :], lhsT=ones_bf[0:1, :], rhs=bghi[0:1, n * 512:(n + 1) * 512], start=False, stop=False)
                        return e.matmul(bk[:, :], lhsT=ones_bf[0:1, :], rhs=bglo[0:1, n * 512:(n + 1) * 512], start=False, stop=True)
                    op("pe", f_gate, r=["featT", KLATE["wgate"], "ones_bf", "bghi", "bglo"], w=[kbg])
                    op("act", (lambda bk, n: (lambda e: e.activation(out=eg[:], in_=bk[:, :], func=AF.Exp, scale=-1.0)))(bkg, n), w=[kbg, "eg"])
                    op("act", lambda e: e.activation(out=eg[:], in_=eg[:], func=AF.Ln, bias=1.0, scale=1.0), w=["eg"])
                    op("act", lambda e: e.activation(out=eg[:], in_=eg[:], func=AF.Exp, scale=-1.0), w=["eg"])
                    bkp, kbp = fbank()
                    def f_ple(e, bk=bkp, n=n):
                        for c in range(2):
                            ins = e.matmul(bk[:, :], lhsT=pT[:, c, :], rhs=wple[:, c, n * 512:(n + 1) * 512], start=(c == 0), stop=(c == 1))
                        return ins
                    op("pe", f_ple, r=["pT", KLATE["wple"]], w=[kbp])
                    op("dve", (lambda bk: (lambda e: e.tensor_tensor(out=eg[:], in0=bk[:, :], in1=eg[:], op=ALU.mult)))(bkp), w=[kbp, "eg"])
                    op("dve", (lambda RR, n: (lambda e: e.tensor_tensor(out=RR[:, n * 512:(n + 1) * 512], in0=RR[:, n * 512:(n + 1) * 512], in1=eg[:], op=ALU.add)))(RR, n),
                       r=["eg"], w=[krr])
                    yield 2.6
                def f_stats2(e):
                    e.bn_stats(out=stats2[:, 0, :], in_=RR[:, 0:512])
                    return e.bn_stats(out=stats2[:, 1, :], in_=RR[:, 512:1024])
                op("dve", f_stats2, r=[krr], w=["stats2"])
                op("dve", lambda e: e.bn_aggr(out=mv2[:], in_=stats2[:]), r=["stats2"], w=["mv2"])
                op("dve", lambda e: e.tensor_scalar_mul(out=nmr2[:], in0=mv2[:, 0:1], scalar1=-1.0), r=["mv2"], w=["nmr2"])
                op("act", lambda e: e.activation(out=rstd2[:], in_=mv2[:, 1:2], func=AF.Ln, bias=LN_EPS, scale=1.0), r=["mv2"], w=["rstd2"])
                op("act", lambda e: e.activation(out=rstd2[:], in_=rstd2[:], func=AF.Exp, scale=-0.5), w=["rstd2"])
                op("act", lambda e: e.activation(out=nmr2[:], in_=nmr2[:], func=AF.Identity, scale=rstd2[:, 0:1]), r=["rstd2"], w=["nmr2"])
                op("act", lambda e: e.activation(out=RR[:], in_=RR[:], func=AF.Identity, scale=rstd2[:, 0:1], bias=nmr2[:, 0:1]), r=["rstd2", "nmr2"], w=[krr])
                op("pool", lambda e: e.tensor_tensor(out=RR[:], in0=RR[:], in1=vecs[:, 1, :], op=ALU.mult), r=["vecs"], w=[krr])
                op("pool", lambda e: e.dma_start(out=RR[:], in_=vec_d[3], accum_op=ALU.add), w=[krr], dma="acc" + krr)
                out_dmas.append(op("sp", lambda e: e.dma_start(out=out_d[i * 128:(i + 1) * 128, :], in_=RR[:]), r=[krr], dma="o" + krr))
                yield 0.5

            return dict(loads=loads, loads_p=loads_p, L=stage_L(), A1=gen_A1(), A2=gen_A2(), B=gen_B(), C=gen_C(), D=gen_D())

        tiles = [make_tile(i) for i in range(NT)]

        def step(g):
            try:
                return next(g)
            except StopIteration:
                return None

        def drain(g):
            while step(g) is not None:
                pass

        def merge(streams):
            for st_ in streams:
                st_["p"] = 0.0
                st_["done"] = False
            tau = 0.0
            while not all(st_["done"] for st_ in streams):
                best, lag = None, 0.0
                for st_ in streams:
                    if st_["done"]:
                        continue
                    if any(not d["done"] for d in st_.get("after", ())):
                        continue
                    tgt = min(1.0, max(0.0, (tau - st_["t0"]) / (st_["t1"] - st_["t0"])))
                    l = tgt - st_["p"] / st_["total"]
                    if l > lag:
                        best, lag = st_, l
                if best is None:
                    if tau > 1.5:
                        for st_ in streams:
                            if not st_["done"]:
                                drain(st_["g"])
                                st_["done"] = True
                        break
                    tau += 0.01
                    continue
                v = step(best["g"])
                if v is None:
                    best["done"] = True
                else:
                    best["p"] += v

        def chain(*gs):
            for g_ in gs:
                while True:
                    v = step(g_)
                    if v is None:
                        break
                    yield v

        for t in range(min(NX, NT)):
            tiles[t]["loads"]()
        drain(tiles[0]["L"])
        if NT > 1:
            for _ in range(4):
                step(tiles[1]["L"])
        step(tiles[0]["A1"])
        if NT > 1:
            drain(tiles[1]["L"])
        drain(tiles[0]["A1"])
        drain(tiles[0]["A2"])
        emit_late()
        for i in range(NT + 1):
            streams = []
            if i >= 1:
                tiles[i - 1]["loads_p"]()
                gD = tiles[i - 1]["D"]
                step(gD)
                streams.append(dict(g=gD, total=10.0, t0=SCHED["D"][0], t1=SCHED["D"][1]))
            if i == 0:
                streams.append(dict(g=gen_W(), total=8.0, t0=0.0, t1=0.9))
            if i < NT:
                gB, gC = tiles[i]["B"], tiles[i]["C"]
                step(gB)
                streams.append(dict(g=gB, total=5.1, t0=SCHED["B"][0], t1=SCHED["B"][1]))
                streams.append(dict(g=gC, total=5.2, t0=SCHED["C"][0], t1=SCHED["C"][1]))
                if i + 1 < NT:
                    gA = chain(tiles[i + 1]["A1"], tiles[i + 1]["A2"])
                    step(gA)
                    streams.append(dict(g=gA, total=12.5, t0=SCHED["A"][0], t1=SCHED["A"][1]))
                if i + 2 < NT:
                    streams.append(dict(g=tiles[i + 2]["L"], total=5.0, t0=SCHED["L"][0], t1=SCHED["L"][1]))
            merge(streams)
            if i >= 1 and i + NX - 1 < NT:
                tiles[i + NX - 1]["loads"]()
        P.emit(nc, final_waits=out_dmas)
    return nc


def _host_consts():
    a = np.arange(128)
    same = (a[:, None] // 64) == (a[None, :] // 64)
    ident = np.eye(128, dtype=np.float32)
    mle = (same & (a[:, None] <= a[None, :])).astype(np.float32)
    mgt = (same & (a[:, None] > a[None, :])).astype(np.float32)
    maskneg = np.where((a[:, None] >= 64) & (a[None, :] < 64), -1e30, 0.0).astype(np.float32)
    cind = np.zeros((128, 128), np.float32)
    cind[:64, 0] = 1.0
    cind[64:, 1] = 1.0
    return np.stack([mle, mgt, cind], 0), np.stack([ident, maskneg], 0)


def _prep_shared(inputs):
    f = np.float32
    vecs = np.stack([np.broadcast_to(np.asarray(inputs[k], f).reshape(1, D), (128, D))
                     for k in ("ln_in_g", "ln_in_b", "ln_g", "ln_b")], 0)
    gnorm = np.asarray(inputs["gla_norm_g"], f)[0].T
    rel = np.asarray(inputs["rel_bias"], f)[0]
    k = np.arange(128)[:, None]
    q = np.arange(128)[None, :]
    idx_prev = np.clip(q + 128 - k, -128, 128) + 128
    idx_diag = np.clip(q - k, -128, 128) + 128
    biasT = np.stack([rel[:, idx_prev], rel[:, idx_diag]], 0)
    biasT = np.ascontiguousarray(np.transpose(biasT, (2, 0, 1, 3)))
    cfar = np.broadcast_to(rel[:, 256].reshape(1, 8), (128, 8))
    wgg = np.concatenate([np.asarray(inputs["w_gla_gate"], f)[0], np.asarray(inputs["b_gla_gate"], f)[0].reshape(1, 256)], 0)
    return {
        "w_in": np.ascontiguousarray(np.asarray(inputs["w_in"], f)[0]),
        "w_out": np.ascontiguousarray(np.asarray(inputs["w_out"], f)[0]),
        "w_gate": np.ascontiguousarray(np.asarray(inputs["w_ple_gate"], f)[0]),
        "w_ple": np.ascontiguousarray(np.asarray(inputs["w_ple"], f)[0]),
        "vecs": np.ascontiguousarray(vecs),
        "gnorm": np.ascontiguousarray(gnorm),
        "bgate": np.ascontiguousarray(np.asarray(inputs["b_ple_gate"], f)[0].reshape(1, D)),
        "wgg": np.ascontiguousarray(wgg),
        "biasT": biasT,
        "cfar": np.ascontiguousarray(cfar),
        "consts": _host_consts()[0],
        "consts2": _host_consts()[1],
    }


def kernel(**inputs):
    x = np.asarray(inputs["x"], np.float32)
    p = np.asarray(inputs["p"], np.float32)[0]
    B, S, _ = x.shape
    NT = S // 128
    shared = _prep_shared(inputs)
    import os
    nc = build_nc(NT, dbg=os.environ.get("KDBG"))
    in_maps = []
    for b in range(B):
        m = dict(shared)
        m["x"] = np.ascontiguousarray(x[b])
        m["p"] = np.ascontiguousarray(p[b])
        in_maps.append(m)
    res = run_bass_kernel_spmd(nc, in_maps, core_ids=list(range(B)))
    out = np.stack([np.asarray(res.results[b]["out"], np.float32).reshape(S, D) for b in range(B)], 0)
    return out
```
